# Optimizing a Trainium2 kernel written in Bass

```python
import jax, jax.numpy as jnp
from jax import lax
import numpy as np

D_MODEL = 1024
BATCH = 16
SEQ = 4096
DEPTH = 2
DEC_BATCH = 16
DEC_SEQ = 16
PAST_LEN = 1024

CHUNK = 64
MIX_WIDTH = D_MODEL
POOL_WINDOWS = (2, 4, 8, 16)
POOL_GROUP_DIM = D_MODEL // 16
POOL_WIDTH = len(POOL_WINDOWS) * POOL_GROUP_DIM
POOL_HIST = max(POOL_WINDOWS) - 1
CONV_WIDTH = 3 * D_MODEL // 8
CONV_K = 3
CONV_HIST = CONV_K - 1
GMLP_WIDTH = 3 * D_MODEL // 8
GMLP_HEADS = 6
GMLP_HEAD_DIM = GMLP_WIDTH // GMLP_HEADS
GMLP_CHUNK = 128
SPLIT_SIZES = (POOL_WIDTH, POOL_WIDTH,
               CONV_WIDTH, CONV_WIDTH, CONV_WIDTH, CONV_WIDTH,
               GMLP_WIDTH, GMLP_WIDTH, GMLP_WIDTH)
PROJ_WIDTH = sum(SPLIT_SIZES)
EPS = 1e-6

kernel_name = "hybrid_pool_conv_gmlp_stream_step"


def rmsnorm(x, g):
    x32 = x.astype(jnp.float32)
    y = x32 * lax.rsqrt(jnp.mean(x32 * x32, axis=-1, keepdims=True) + EPS)
    return (y * g.astype(jnp.float32)).astype(x.dtype)


def layernorm(x, g, b):
    x32 = x.astype(jnp.float32)
    mu = jnp.mean(x32, axis=-1, keepdims=True)
    var = jnp.mean(jnp.square(x32 - mu), axis=-1, keepdims=True)
    y = (x32 - mu) * lax.rsqrt(var + EPS)
    return (y * g.astype(jnp.float32) + b.astype(jnp.float32)).astype(x.dtype)


def pool_mix(p, hist, pos0, w_pool, pool_scale):
    L = p.shape[1]
    full = jnp.concatenate([hist, p], axis=1)
    cs = jnp.cumsum(full.astype(jnp.float32), axis=1)
    cs = jnp.pad(cs, ((0, 0), (1, 0), (0, 0)))
    pos = pos0 + jnp.arange(L, dtype=jnp.int32)
    outs = []
    for g, w in enumerate(POOL_WINDOWS):
        sl = slice(g * POOL_GROUP_DIM, (g + 1) * POOL_GROUP_DIM)
        wsum = cs[:, POOL_HIST + 1:, sl] - cs[:, POOL_HIST + 1 - w:POOL_HIST + 1 - w + L, sl]
        cnt = jnp.minimum(pos + 1, w).astype(jnp.float32)
        d = (wsum / cnt[None, :, None]).astype(p.dtype) - p[..., sl]
        outs.append(jnp.einsum('bld,de->ble', d, w_pool[g]))
    out = jnp.concatenate(outs, axis=-1) * pool_scale
    return out, full[:, -POOL_HIST:]


def conv_mix(q, hist, w_conv):
    L = q.shape[1]
    full = jnp.concatenate([hist, q], axis=1)
    out = sum(w_conv[k] * full[:, k:k + L] for k in range(CONV_K))
    return out, full[:, -CONV_HIST:]


def gmlp_mix(u, v, w_s, b_s):
    B, L, _ = v.shape
    n = min(L, GMLP_CHUNK)
    blk = jnp.arange(n) // CHUNK
    mask = blk[None, :] <= blk[:, None]
    w = jnp.where(mask[None], w_s[:, :n, :n], 0)
    vh = v.reshape(B, L // n, n, GMLP_HEADS, GMLP_HEAD_DIM)
    mixed = jnp.einsum('hts,bcshd->bcthd', w, vh) + b_s[:, :n].T[None, None, :, :, None]
    return u * mixed.reshape(B, L, GMLP_WIDTH)


def layer(x, c, pool_hist, conv_hist, pos0, norm_g, w_ada, b_ada, w_in, w_pool, pool_scale,
          w_conv, v_norm_g, v_norm_b, w_s, b_s, w_out):
    mod = jax.nn.silu(c) @ w_ada + b_ada
    shift, scale, gate = jnp.split(mod, 3, axis=-1)
    h = rmsnorm(x, norm_g) * (1 + scale[:, None]) + shift[:, None]
    z = h @ w_in
    p, z_pool, gb, gc, hx, z_conv, u, v, z_gmlp = jnp.split(
        z, list(np.cumsum(SPLIT_SIZES)[:-1]), axis=-1)
    y_pool, new_pool = pool_mix(p, pool_hist, pos0, w_pool, pool_scale)
    y_conv, new_conv = conv_mix(gc * hx, conv_hist, w_conv)
    vn = layernorm(v, v_norm_g, v_norm_b)
    y_gmlp = gmlp_mix(u, vn, w_s, b_s)
    y = jnp.concatenate([y_pool * jax.nn.silu(z_pool),
                         gb * y_conv * jax.nn.silu(z_conv),
                         y_gmlp * jax.nn.silu(z_gmlp)], axis=-1) @ w_out
    return x + gate[:, None] * y, new_pool, new_conv, vn


def setup_inputs(seed: int = 0) -> dict:
    key = jax.random.key(seed)
    ks = jax.random.split(key, 20)
    nrm = lambda k, s, sc: jax.random.normal(k, s, jnp.float32) * sc
    D = D_MODEL
    return {
        "x_prompt": nrm(ks[0], (BATCH, SEQ, D), 1.0),
        "x_sample": nrm(ks[1], (DEC_BATCH, DEC_SEQ, D), 1.0),
        "state_pool": nrm(ks[2], (DEPTH, DEC_BATCH, POOL_HIST, POOL_WIDTH), 1.0),
        "state_conv": nrm(ks[3], (DEPTH, DEC_BATCH, CONV_HIST, CONV_WIDTH), 1.0),
        "c_prompt": nrm(ks[4], (BATCH, D), 1.0),
        "c_sample": nrm(ks[5], (DEC_BATCH, D), 1.0),
        "norm_g": 1.0 + nrm(ks[6], (DEPTH, D), 0.05),
        "w_ada": nrm(ks[7], (DEPTH, D, 3 * D), 0.5 * D ** -0.5),
        "b_ada": nrm(ks[8], (DEPTH, 3 * D), 0.02),
        "w_in": nrm(ks[9], (DEPTH, D, PROJ_WIDTH), D ** -0.5),
        "w_pool": nrm(ks[10], (DEPTH, len(POOL_WINDOWS), POOL_GROUP_DIM, POOL_GROUP_DIM), POOL_GROUP_DIM ** -0.5),
        "pool_scale": 1.0 + nrm(ks[11], (DEPTH, POOL_WIDTH), 0.1),
        "w_conv": nrm(ks[12], (DEPTH, CONV_K, CONV_WIDTH), CONV_K ** -0.5),
        "v_norm_g": 1.0 + nrm(ks[13], (DEPTH, GMLP_WIDTH), 0.05),
        "v_norm_b": nrm(ks[14], (DEPTH, GMLP_WIDTH), 0.02),
        "w_s": nrm(ks[15], (DEPTH, GMLP_HEADS, GMLP_CHUNK, GMLP_CHUNK), GMLP_CHUNK ** -0.5),
        "b_s": 1.0 + nrm(ks[16], (DEPTH, GMLP_HEADS, GMLP_CHUNK), 0.1),
        "w_out": nrm(ks[17], (DEPTH, MIX_WIDTH, D), MIX_WIDTH ** -0.5),
        "final_norm_g": 1.0 + nrm(ks[18], (D,), 0.05),
    }


def reference(x_prompt, x_sample, state_pool, state_conv, c_prompt, c_sample, norm_g, w_ada, b_ada,
              w_in, w_pool, pool_scale, w_conv, v_norm_g, v_norm_b, w_s, b_s, w_out, final_norm_g):
    xp, xs = x_prompt, x_sample
    zero_pool = jnp.zeros((xp.shape[0], POOL_HIST, POOL_WIDTH), xp.dtype)
    zero_conv = jnp.zeros((xp.shape[0], CONV_HIST, CONV_WIDTH), xp.dtype)
    pool_p, conv_p, pool_s, conv_s, v_s = [], [], [], [], []
    for l in range(DEPTH):
        params = (norm_g[l], w_ada[l], b_ada[l], w_in[l], w_pool[l], pool_scale[l], w_conv[l],
                  v_norm_g[l], v_norm_b[l], w_s[l], b_s[l], w_out[l])
        xp, npool, nconv, _ = layer(xp, c_prompt, zero_pool, zero_conv, 0, *params)
        pool_p.append(npool)
        conv_p.append(nconv)
        xs, npool, nconv, vrows = layer(xs, c_sample, state_pool[l], state_conv[l], PAST_LEN, *params)
        pool_s.append(npool)
        conv_s.append(nconv)
        v_s.append(vrows)
    y_prompt = rmsnorm(xp, final_norm_g)
    y_sample = rmsnorm(xs, final_norm_g)
    return (y_prompt, y_sample, jnp.stack(pool_p), jnp.stack(conv_p), jnp.stack(pool_s),
            jnp.stack(conv_s), jnp.stack(v_s))
```

```python
import contextlib
import numpy as np
import concourse.bass as bass
import concourse.mybir as mybir
from concourse.bass_utils import run_bass_kernel_spmd

F32 = mybir.dt.float32
BF16 = mybir.dt.bfloat16
ALU = mybir.AluOpType
AF = mybir.ActivationFunctionType

D = 1024
DEPTH = 2
PW = 3200
EPS = 1e-6
GS = 1.702
TT = 256
NXB = 2
NHT = 1
EXPERIMENT_SHRINK = False
REORDER = True
PRIO_CP = True
PRIO_BUCKET = 15
HOP_NS = 400.0
HOP_X = 500.0
HOP_SAME = 200.0
C_P, C_ZP, C_GB, C_GC, C_HX, C_ZC, C_U, C_V, C_ZG = 0, 256, 512, 896, 1280, 1664, 2048, 2432, 2816


class Op:
    __slots__ = ("idx", "eng", "fn", "deps", "order", "inc", "dur", "lat", "dsem", "seq", "dcount", "done")

    def __init__(self, idx, eng, fn, deps, inc, dur, lat=0.0, dsem=None):
        self.idx, self.eng, self.fn, self.deps, self.inc = idx, eng, fn, deps, inc
        self.order = []
        self.dur, self.lat, self.dsem = dur, lat, dsem
        self.seq = 0
        self.dcount = 0
        self.done = 0.0


class Prog:
    ENGS = ("pe", "act", "dve", "pool", "sp")

    def __init__(self, nc, stack):
        self.nc = nc
        self.stack = stack
        self.all = []
        self.sem = {e: stack.enter_context(nc.semaphore("c_" + e)) for e in self.ENGS}
        self.buf = {}
        self.dsem = {}
        self.dlast = {}
        self.groups = {}

    def _deps(self, eng, reads, writes):
        deps = []
        for k in reads:
            st = self.buf.get(k)
            if st and st[0] is not None:
                deps.append(st[0])
        for k in writes:
            st = self.buf.get(k)
            if st:
                if st[0] is not None:
                    deps.append(st[0])
                deps.extend(st[1].values())
        return deps

    def _mark(self, me, reads, writes):
        for k in reads:
            st = self.buf.setdefault(k, [None, {}])
            st[1][id(me)] = me
        for k in writes:
            self.buf[k] = [me, {}]

    def op(self, eng, fn, reads=(), writes=(), n=256, dur=None):
        if dur is None:
            dur = {"act": (130 + 0.55 * n) if n <= 512 else (200 + 1.0 * n), "dve": 200 + 0.85 * n,
                   "pool": 180 + 1.8 * n, "pe": 60 + n / 2.0}[eng]
        deps = self._deps(eng, reads, writes)
        o = Op(len(self.all), eng, fn, deps, True, dur)
        self.all.append(o)
        self._mark(o, reads, writes)
        return o

    def dma(self, q, name, out, in_, reads=(), writes=(), batch=None, nbytes=65536, **kw):
        if name not in self.dsem:
            self.dsem[name] = self.stack.enter_context(self.nc.semaphore("d_" + name))
        deps = self._deps(q, reads, writes)

        def fn(e, out=out, in_=in_, kw=kw):
            return e.dma_start(out=out, in_=in_, **kw)
        o = Op(len(self.all), q, fn, deps, False, 150.0 if q == "sp" else 1200.0, 2000.0 + nbytes / 250.0, name)
        if batch is not None:
            g = self.groups.setdefault((name, batch), [])
            o.deps = [d for d in deps if d not in g]
            g.append(o)
        prev = self.dlast.get(name)
        if prev is not None:
            o.order.append(prev)
        self.dlast[name] = o
        self.all.append(o)
        self._mark(o, reads, writes)
        return o

    def schedule(self):
        import heapq
        ops = self.all
        member = {}
        for g in self.groups.values():
            for o in g:
                member[id(o)] = g
        for o in ops:
            extra = []
            for d in o.deps:
                g = member.get(id(d))
                if g is not None and o not in g:
                    extra.extend(g)
            if extra:
                o.deps = list(o.deps) + extra
        nsucc = [[] for _ in ops]
        indeg = [0] * len(ops)
        for o in ops:
            ds = set(id(d) for d in o.deps) | set(id(d) for d in o.order)
            preds = {d.idx: d for d in list(o.deps) + list(o.order)}
            indeg[o.idx] = len(preds)
            for d in preds.values():
                nsucc[d.idx].append(o)
        heaps = {e: [] for e in self.ENGS}
        prio = [0.0] * len(ops)
        if PRIO_CP:
            for o in reversed(ops):
                best = 0.0
                for s_ in nsucc[o.idx]:
                    if prio[s_.idx] > best:
                        best = prio[s_.idx]
                prio[o.idx] = best + o.dur + (o.lat if o.dsem else 120.0)
        ready_t = [0.0] * len(ops)
        for o in ops:
            if indeg[o.idx] == 0:
                heapq.heappush(heaps[o.eng], (0.0, o.idx))
        t_eng = {e: 0.0 for e in self.ENGS}
        self.gorder = []
        dma_free = 0.0
        order = {e: [] for e in self.ENGS}
        nleft = len(ops)
        while nleft:
            best = None
            for e in self.ENGS:
                h = heaps[e]
                if not h:
                    continue
                rt, idx = h[0]
                if rt <= t_eng[e]:
                    cand = []
                    while h and h[0][0] <= t_eng[e]:
                        cand.append(heapq.heappop(h))
                    if PRIO_CP:
                        pick = min(cand, key=lambda c: (c[1] // PRIO_BUCKET, -prio[c[1]], c[1]))
                    else:
                        pick = min(cand, key=lambda c: c[1])
                    for c in cand:
                        if c is not pick:
                            heapq.heappush(h, c)
                    start = t_eng[e]
                else:
                    pick = heapq.heappop(h)
                    start = rt
                if best is None or start < best[0]:
                    if best is not None:
                        heapq.heappush(heaps[best[1]], best[2])
                    best = (start, e, pick)
                else:
                    heapq.heappush(h, pick)
            start, e, pick = best
            o = ops[pick[1]]
            self.gorder.append(o)
            t_eng[e] = start + o.dur
            if o.dsem:
                xfer = (o.lat - 2000.0) * 250.0 / 270.0
                dma_free = max(dma_free, start) + xfer
                o.done = max(start + 2000.0, dma_free + 1000.0)
            else:
                o.done = start + o.dur
            order[e].append(o)
            nleft -= 1
            for s in nsucc[o.idx]:
                hop = 0.0 if o.dsem else (HOP_SAME if s.eng == o.eng else HOP_X)
                ready_t[s.idx] = max(ready_t[s.idx], o.done + hop)
                indeg[s.idx] -= 1
                if indeg[s.idx] == 0:
                    heapq.heappush(heaps[s.eng], (ready_t[s.idx], s.idx))
        self.est_ns = max(t_eng.values())
        return order

    def emit(self, final_sems=("yout", "stop", "stoc", "nvo"), reorder=True):
        nc = self.nc
        if reorder:
            order = self.schedule()
        else:
            order = {e: [o for o in self.all if o.eng == e] for e in self.ENGS}
        dcnt = {}
        for e in self.ENGS:
            c = 0
            for o in order[e]:
                if o.dsem:
                    dcnt[o.dsem] = dcnt.get(o.dsem, 0) + 16
                    o.dcount = dcnt[o.dsem]
                else:
                    c += 1
                    o.seq = c
        for g in self.groups.values():
            m = max(o.dcount for o in g)
            for o in g:
                o.dcount = m
        gorder = self.gorder if reorder else list(self.all)
        know = {e: {} for e in self.ENGS}
        clock = {}
        waits = {}
        for o in gorder:
            e = o.eng
            K = know[e]
            needs = []
            for d in o.deps:
                if d.dsem:
                    needs.append((("d", d.dsem), self.dsem[d.dsem], d.dcount, d))
                else:
                    if d.eng == e and e == "pe":
                        continue
                    needs.append((("e", d.eng), self.sem[d.eng], d.seq, d))
            w = []
            for key, sobj, v, d in sorted(needs, key=lambda t: -t[2]):
                if K.get(key, 0) < v:
                    w.append((sobj, v))
                    K[key] = v
                    for k2, v2 in clock[id(d)].items():
                        if K.get(k2, 0) < v2:
                            K[k2] = v2
            waits[id(o)] = w
            c = dict(K)
            if o.dsem:
                if c.get(("d", o.dsem), 0) < o.dcount:
                    c[("d", o.dsem)] = o.dcount
            else:
                c[("e", e)] = max(c.get(("e", e), 0), o.seq)
            clock[id(o)] = c
        self.n_waits = sum(len(w) for w in waits.values())
        handles = {"pe": "tensor", "act": "scalar", "dve": "vector", "pool": "gpsimd", "sp": "sync"}
        with nc.Block() as block:
            for ename in self.ENGS:
                ops = order[ename]
                if not ops:
                    continue

                def body(e, ops=ops, ename=ename):
                    own = self.sem[ename]
                    for o in ops:
                        for sobj, v in waits[id(o)]:
                            e.wait_ge(sobj, v)
                        ins = o.fn(e)
                        if o.dsem:
                            ins.then_inc(self.dsem[o.dsem], 16)
                        else:
                            ins.then_inc(own, 1)
                    if ename == "sp":
                        for n in final_sems:
                            if n in dcnt:
                                e.wait_ge(self.dsem[n], dcnt[n])
                getattr(block, handles[ename])(body)


def build_nc(SEQ=4096, NS=16):
    nc = bass.Bass("TRN2", target_bir_lowering=False)
    dt_in = lambda n, s: nc.dram_tensor(n, s, F32, kind="ExternalInput").ap()
    dt_out = lambda n, s: nc.dram_tensor(n, s, F32, kind="ExternalOutput").ap()
    xp = dt_in("xp", [2, SEQ, D])
    xs = dt_in("xs", [2, NS, D])
    stp = dt_in("stp", [DEPTH, 2, 15, 256])
    stc = dt_in("stc", [DEPTH, 2, 2, 384])
    c4 = dt_in("c4", [4, D])
    norm_g = dt_in("norm_g", [DEPTH, D])
    w_ada = dt_in("w_ada", [DEPTH, D, 3 * D])
    b_ada = dt_in("b_ada", [DEPTH, 3 * D])
    w_in = dt_in("w_in", [DEPTH, D, PW])
    w_pool = dt_in("w_pool", [DEPTH, 4, 64, 64])
    pool_scale = dt_in("pool_scale", [DEPTH, 256])
    w_conv = dt_in("w_conv", [DEPTH, 3, 384])
    v_norm_g = dt_in("v_norm_g", [DEPTH, 384])
    v_norm_b = dt_in("v_norm_b", [DEPTH, 384])
    w_s = dt_in("w_s", [DEPTH, 6, 128, 128])
    b_s = dt_in("b_s", [DEPTH, 6, 128])
    w_out = dt_in("w_out", [DEPTH, D, D])
    fng = dt_in("final_norm_g", [1, D])
    y_p = dt_out("y_p", [2, SEQ, D])
    y_s = dt_out("y_s", [2, NS, D])
    npp = dt_out("npp", [DEPTH, 2, 15, 256])
    ncp = dt_out("ncp", [DEPTH, 2, 2, 384])
    nps = dt_out("nps", [DEPTH, 2, 15, 256])
    ncs = dt_out("ncs", [DEPTH, 2, 2, 384])
    nvs = dt_out("nvs", [DEPTH, 2, NS, 384])
    gscr = nc.dram_tensor("gscr", [DEPTH, 4, D], F32, kind="Internal").ap()

    with contextlib.ExitStack() as st:
        P = Prog(nc, st)
        sb = lambda n, s, d=F32: st.enter_context(nc.sbuf_tensor(n, s, d))
        nc_ncd = st.enter_context(nc.allow_non_contiguous_dma(reason="tiny constant layouts"))
        B = [st.enter_context(nc.psum_tensor("B%d" % i, [128, 512], F32)) for i in range(8)]
        Bb = [b[:].bitcast(BF16) for b in B]
        BK = ["B%d" % i for i in range(8)]

        win = [sb("win%d" % l, [128, 8, PW], BF16) for l in range(DEPTH)]
        wout = [sb("wout%d" % l, [128, 8, D], BF16) for l in range(DEPTH)]
        xbuf = [sb("xbuf%d" % i, [128, 2, D]) for i in range(NXB)]
        xn = sb("xn", [128, 2, D], BF16)
        otmp = xn[:].rearrange("p s d -> p (s d)").bitcast(F32)
        hT = [sb("hT%d" % i, [128, 8, TT], BF16) for i in range(NHT)]
        ycT = [sb("ycT0", [128, 8, TT], BF16)]
        gate_bc = sb("gate_bc", [128, DEPTH, D])
        gfin = sb("gfin", [128, D]) if not EXPERIMENT_SHRINK else gate_bc[:, 0, :]
        pbuf = sb("pbuf", [128, 2, 15 + TT])
        sA = sb("sA", [128, 1, 15 + TT])
        sB = sb("sB", [128, 1, 15 + TT])
        dT = sb("dT", [128, 2, TT], BF16)
        sgz = sb("sgz", [128, 2, TT])
        phist = sb("phist", [128, DEPTH, 2, 15])
        hxs = sb("hxs", [128, 1, TT])
        qbuf = sb("qbuf", [128, 1, 2 + TT])
        acc = sb("acc", [128, 1, TT])
        sgc = sb("sgc", [128, 1, TT])
        qhist = sb("qhist", [128, DEPTH, 3, 2])
        sgug = [sb("sgug%d" % i, [128, 384]) for i in range(2)]
        nv = [sb("nv%d" % i, [128, 384]) for i in range(2)]
        vnb = [sb("vnb%d" % i, [128, 384], BF16) for i in range(2)]
        yg = [sb("yg%d" % i, [128, 384], BF16) for i in range(2)]
        bst = sb("bst", [128, 2, 6])
        mv = sb("mv", [128, 2, 2])
        rsv = sb("rsv", [128, 2])
        ms = sb("ms", [128, 2])
        msf = sb("msf", [128, 4])
        rsf = sb("rsf", [128, 2])
        rs = sb("rs", [128, 2])
        ident = sb("ident", [128, 128], BF16)
        identf = sb("identf", [128, 128])
        nhalf = sb("nhalf", [128, 1])
        barr = sb("barr", [128, 1])
        WT = [sb("WT%d" % l, [128, 6, 128], BF16) for l in range(DEPTH)]
        wstg = xbuf[0][:, 1, 0:768].rearrange("p (h s) -> p h s", h=6)
        wpbd = [sb("wpbd%d" % l, [128, 2, 128], BF16) for l in range(DEPTH)]
        gam = sb("gam", [128, DEPTH, 384])
        bet = sb("bet", [128, DEPTH, 384])
        bsb = sb("bsb", [128, DEPTH, 384])
        bs6 = sb("bs6", [128, DEPTH, 6])
        psc = sb("psc", [128, DEPTH, 2])
        wc = sb("wc", [128, DEPTH, 3, 3])
        ngT = sb("ngT", [128, DEPTH, 8])
        invw = sb("invw", [128, 2])
        invcnt = sb("invcnt", [128, 2, 15])
        gsT = sb("gsT", [128, DEPTH, 8, 4])
        shT = sb("shT", [128, DEPTH, 8, 4])
        c4t = xbuf[0][0:4, 0, :]
        scT = sb("scT", [128, 8, 4], BF16)
        mrow = otmp[0:4, 0:512].rearrange("p (a b) -> p a b", a=2)
        bblk = otmp[0:4, 512:1024].rearrange("p (a b) -> p a b", a=2)
        stgp = hxs[0:16, 0, :]
        stgc = pbuf[0:2, :, :].rearrange("p a b -> p (a b)")[:, 0:384]

        def pe_group(fns, reads, writes, n):
            def fn(e, fns=fns):
                ins = None
                for f in fns:
                    ins = f(e)
                return ins
            return P.op("pe", fn, reads=reads, writes=writes, dur=8.0 * len(fns) + n / 2.3)

        P.op("pool", lambda e: e.memset(identf[:], 0.0), writes=["identf"])
        P.op("pool", lambda e: e.affine_select(out=identf[:], in_=identf[:], pattern=[[-1, 128]],
                                               compare_op=ALU.not_equal, fill=1.0, base=0,
                                               channel_multiplier=1),
             reads=["identf"], writes=["identf"])
        P.op("dve", lambda e: e.tensor_copy(ident[:], identf[:]), reads=["identf"], writes=["ident"])
        P.op("pool", lambda e: e.memset(nhalf[:], -0.5), writes=["nhalf"])
        wins = (2, 4, 8, 16)
        for g, w in enumerate(wins):
            j, h0 = g // 2, (g % 2) * 64
            P.op("pool", lambda e, j=j, h0=h0, w=w: e.memset(invw[h0:h0 + 64, j:j + 1], 1.0 / w), writes=["invw"], n=1)
            P.op("pool", lambda e, j=j, h0=h0, w=w: e.memset(invcnt[h0:h0 + 64, j, :], 1.0 / w), writes=["invcnt"], n=15)
            for pos in range(w - 1):
                P.op("pool", lambda e, j=j, h0=h0, pos=pos: e.memset(invcnt[h0:h0 + 64, j, pos:pos + 1], 1.0 / (pos + 1)),
                     writes=["invcnt"], n=1)
        P.dma("sp", "cst", c4t[:], c4, writes=["x0_0"], batch=0)
        P.dma("sp", "cst", gfin[:], fng.broadcast_to([128, D]), writes=["gfin"], batch=0)
        for l in range(DEPTH):
            P.dma("sp", "cst", gam[:, l, :], v_norm_g[l:l + 1, :].broadcast_to([128, 384]), writes=["gam"], batch=0)
            P.dma("sp", "cst", bet[:, l, :], v_norm_b[l:l + 1, :].broadcast_to([128, 384]), writes=["bet"], batch=0)
            P.dma("sp", "cst", bs6[:, l, :], b_s[l].rearrange("h t -> t h"), writes=["bs6"], batch=0)
            P.dma("sp", "cst", psc[:, l, :], pool_scale[l].rearrange("(j p) -> p j", p=128), writes=["psc"], batch=0)
            for j in range(3):
                P.dma("sp", "cst", wc[:, l, j, :], w_conv[l][:, j * 128:(j + 1) * 128].rearrange("k p -> p k"), writes=["wc"], batch=0)
            P.dma("sp", "cst", ngT[:, l, :], norm_g[l].rearrange("(k p) -> p k", p=128), writes=["ngT"], batch=0)
        P.op("dve", lambda e: e.tensor_scalar_mul(psc[:], psc[:], GS), reads=["psc"], writes=["psc"], n=4)
        P.op("dve", lambda e: e.tensor_scalar_mul(wc[:], wc[:], GS), reads=["wc"], writes=["wc"], n=18)
        for l in range(DEPTH):
            P.op("dve", lambda e, l=l: e.tensor_copy(bsb[:, l, :].rearrange("p (h c) -> p h c", h=6),
                                                      bs6[:, l, :].unsqueeze(2).to_broadcast([128, 6, 64])),
                 reads=["bs6"], writes=["bsb"], n=384)
        P.op("act", lambda e: e.activation(out=c4t[:], in_=c4t[:], func=AF.Gelu_apprx_sigmoid, scale=1.0 / GS),
             reads=["x0_0"], writes=["x0_0"], n=1024)
        pe_group([lambda e, k=k: e.transpose(B[0][:, k * 4:(k + 1) * 4], c4t[0:4, k * 128:(k + 1) * 128], identf[0:4, 0:4])
                  for k in range(8)], ["x0_0", "identf"], [BK[0]], 128)
        P.op("dve", lambda e: e.tensor_copy(scT[:].rearrange("p k s -> p (k s)"), B[0][:, 0:32]), reads=[BK[0]], writes=["scT"], n=32)
        blk = 0
        for l in range(DEPTH):
            for jb in range(12):
                par = blk % 2
                HT0K = ["hT0_%d_%d" % (s_, k_) for s_ in range(2) for k_ in range(8)]
                YC0K = ["ycT0_%d" % c_ for c_ in range(5)] + ["ycT0_g0", "ycT0_g1"]
                x1v = xbuf[1][:].rearrange("p s d -> p (s d)").bitcast(BF16)
                stages = [(hT[0], HT0K), (ycT[0], YC0K),
                          (x1v[:, 0:2048].rearrange("p (k n) -> p k n", k=8), ["x1_0"]),
                          (x1v[:, 2048:4096].rearrange("p (k n) -> p k n", k=8), ["x1_1"])]
                sbuf_stage, skeys = stages[blk % 4]
                spar = blk % 4
                bx, by = 2 + par, 4 + par
                c0 = jb * 256
                last_wada = P.dma("pool", "wada%d" % spar, sbuf_stage[:], w_ada[l][:, c0:c0 + 256].rearrange("(k p) n -> p k n", p=128),
                                  writes=skeys, nbytes=1048576)
                P.dma("sp", "bblk%d" % par, bblk[:, par, :], b_ada[l:l + 1, c0:c0 + 256].broadcast_to([4, 256]),
                      writes=["bblk%d" % par])
                pe_group([lambda e, k=k, bx=bx, s_=sbuf_stage: e.matmul(B[bx][0:4, 0:256], scT[:, k, :], s_[:, k, :],
                                                                       start=(k == 0), stop=(k == 7)) for k in range(8)],
                         ["scT"] + skeys, [BK[bx]], 8 * 256)
                P.op("dve", lambda e, bx=bx, par=par: e.scalar_tensor_tensor(out=mrow[:, par, :], in0=B[bx][0:4, 0:256], scalar=GS,
                                                                          in1=bblk[:, par, :], op0=ALU.mult, op1=ALU.add),
                     reads=[BK[bx], "bblk%d" % par], writes=["mrow%d" % par])
                if jb < 8:
                    pe_group([lambda e, i=i, by=by, par=par: e.transpose(B[by][:, i * 4:(i + 1) * 4],
                                                                      mrow[0:4, par, i * 128:(i + 1) * 128], identf[0:4, 0:4])
                              for i in range(2)], ["mrow%d" % par, "identf"], [BK[by]], 64)
                    if jb < 4:
                        P.op("dve", lambda e, l=l, jb=jb, by=by: e.tensor_copy(
                            shT[:, l, 2 * jb:2 * jb + 2, :], B[by][:, 0:8].rearrange("p (i s) -> p i s", i=2)),
                            reads=[BK[by]], writes=["shT"], n=8)
                    else:
                        for i in range(2):
                            kk = 2 * (jb - 4) + i
                            P.op("dve", lambda e, l=l, kk=kk, i=i, by=by: e.scalar_tensor_tensor(
                                out=gsT[:, l, kk, :], in0=B[by][:, i * 4:(i + 1) * 4], scalar=1.0,
                                in1=ngT[:, l, kk:kk + 1].to_broadcast([128, 4]), op0=ALU.add, op1=ALU.mult),
                                reads=[BK[by], "ngT"], writes=["gsT"], n=4)
                else:
                    P.dma("sp", "gscr%d" % par, gscr[l, :, (jb - 8) * 256:(jb - 7) * 256], mrow[:, par, :],
                          reads=["mrow%d" % par], writes=["gscr%d%d" % (l, jb)])
                blk += 1

        for l in range(DEPTH):
            P.dma("sp", "x0_1", wstg[:], w_s[l].rearrange("h t s -> t h s"), reads=[], writes=["x0_1"], nbytes=393216)
            for h in range(6):
                bk = 2 + (h % 2)
                pe_group([lambda e, h=h, bk=bk: e.transpose(B[bk][:, 0:128], wstg[:, h, :], identf[:])],
                         ["x0_1", "identf"], [BK[bk]], 512)
                P.op("dve", lambda e, l=l, h=h, bk=bk: e.tensor_copy(WT[l][:, h, :], B[bk][:, 0:128]),
                     reads=[BK[bk]], writes=["WT%d" % l], n=128)
            P.op("pool", lambda e, l=l: e.memset(WT[l][64:128, :, 0:64], 0.0), reads=["WT%d" % l], writes=["WT%d" % l], n=384)
            P.op("pool", lambda e: e.memset(wstg[:, 0:2, :], 0.0), writes=["x0_1"], n=256)
            for g in range(4):
                j, h0 = g // 2, (g % 2) * 64
                P.dma("sp", "x0_1", wstg[h0:h0 + 64, j, h0:h0 + 64], w_pool[l, g], reads=["x0_1"], writes=["wstgd%d" % g])
            P.op("dve", lambda e, l=l: e.tensor_copy(wpbd[l][:], wstg[:, 0:2, :]),
                 reads=["x0_1", "wstgd0", "wstgd1", "wstgd2", "wstgd3"], writes=["wpbd%d" % l], n=256)

        P.op("pool", lambda e: e.memset(barr[:], 0.0), writes=["mrow0", "mrow1", "bblk0", "bblk1", "xn0", "xn1"], n=1)

        prev = last_wada
        for l in range(DEPTH):
            for k in range(8):
                o = P.dma("pool", "win%d" % l, win[l][:, k, :], w_in[l][k * 128:(k + 1) * 128, :], writes=["win%d_%d" % (l, k)],
                          batch=0, nbytes=1638400)
                o.order.append(prev)
                prev = o
            for k in range(0, 8, 4):
                o = P.dma("pool", "wout%d" % l, wout[l][:, k:k + 4, :],
                          w_out[l][k * 128:(k + 4) * 128, :].rearrange("(k p) n -> p k n", p=128), writes=["wout%d_%d" % (l, k // 4)],
                          batch=0, nbytes=2097152)
                o.order.append(prev)
                prev = o

        def rstd_op(src, dst, reads, writes):
            P.op("pool", lambda e: e.tensor_scalar_add(dst, src, EPS), reads=reads, writes=writes, n=1)
            P.op("pool", lambda e: e.tensor_tensor(dst, dst, nhalf[:dst.shape[0], :], ALU.pow),
                 reads=writes + ["nhalf"], writes=writes, dur=490.0)

        def gas(out, in_):
            return lambda e: e.activation(out=out, in_=in_, func=AF.Gelu_apprx_sigmoid, scale=1.0 / GS)

        import os as _os2
        KPH = int(_os2.environ.get("KPH", "100"))
        KQ = int(_os2.environ.get("KQ", "100"))

        def tile_layer(seq, l, NT, subs, first, emit_state, is_sample, last_layer, out_ap, t0, pp, gslot=None):
            if gslot is None:
                gslot = l
            H = 15
            xb, hTp, ycTp = xbuf[pp], hT[pp % NHT], ycT[0]
            XK = ["x%d_%d" % (pp, s) for s in range(2)]
            HK, YK = "hT%d" % (pp % NHT), "ycT0"
            WK = ["win%d_%d" % (l, k) for k in range(8)]

            def fm_mm(bank, off, c0):
                pe_group([lambda e, k=k: e.matmul(B[bank][:, off:off + NT], win[l][:, k, c0:c0 + 128], hTp[:, k, 0:NT],
                                                  start=(k == 0), stop=(k == 7)) for k in range(8)],
                         WK + [HK], [BK[bank]], 8 * NT)

            for s, (r0, nr) in enumerate(subs):
                xk = XK[s]
                P.op("act", lambda e, s=s, nr=nr: e.activation(out=xn[:nr, s, :], in_=xb[:nr, s, :], func=AF.Square,
                                                             scale=1.0 / 32.0, accum_out=ms[:nr, s:s + 1]),
                     reads=[xk], writes=["xn%d" % s, "ms%d" % s], n=1024)
                if KQ <= 0:
                    continue
                rstd_op(ms[:nr, s:s + 1], rs[:nr, s:s + 1], ["ms%d" % s], ["rs%d" % s])
                if KQ <= 1:
                    continue
                P.op("act", lambda e, s=s, nr=nr: e.activation(out=xn[:nr, s, :], in_=xb[:nr, s, :], func=AF.Copy,
                                                             scale=rs[:nr, s:s + 1]),
                     reads=[xk, "rs%d" % s], writes=["xn%d" % s], n=1024)
                if KQ <= 2:
                    continue
                tbank = (2 + s, 6 + s)
                for hb in range(2):
                    pe_group([lambda e, s=s, nr=nr, k=k, tb=tbank[hb]: e.transpose(Bb[tb][:, (k % 4) * 128:(k % 4) * 128 + nr],
                                                                            xn[:nr, s, k * 128:(k + 1) * 128], ident[:nr, :nr])
                              for k in range(4 * hb, 4 * hb + 4)], ["xn%d" % s, "ident"], [BK[tbank[hb]]], 4 * 128)
                if KQ <= 3:
                    continue
                for k in range(8):
                    hb = k // 4
                    o = hTp[:, k, r0:r0 + nr]
                    i_ = Bb[tbank[hb]][:, (k % 4) * 128:(k % 4) * 128 + nr]
                    g_ = gsT[:, l, k, seq:seq + 1]
                    h_ = shT[:, l, k, seq:seq + 1]
                    if hb == 0:
                        P.op("act", lambda e, o=o, i_=i_, g_=g_, h_=h_: e.activation(out=o, in_=i_, func=AF.Identity, bias=h_, scale=g_),
                             reads=[BK[tbank[hb]], "gsT", "shT"], writes=[HK + "_%d_%d" % (s, k)], n=nr)
                    else:
                        P.op("dve", lambda e, o=o, i_=i_, g_=g_, h_=h_: e.tensor_scalar(o, i_, g_, h_, ALU.mult, ALU.add),
                             reads=[BK[tbank[hb]], "gsT", "shT"], writes=[HK + "_%d_%d" % (s, k)], n=nr)
            HKS = [[HK + "_%d_%d" % (s, k) for k in range(8)] for s in range(len(subs))]
            HKA = [k_ for ks in HKS for k_ in ks]

            def fm_mm(bank, off, c0):
                pe_group([lambda e, k=k: e.matmul(B[bank][:, off:off + NT], win[l][:, k, c0:c0 + 128], hTp[:, k, 0:NT],
                                                  start=(k == 0), stop=(k == 7)) for k in range(8)],
                         WK + HKA, [BK[bank]], 8 * NT)

            tmb = [(6, 7, 2), (0, 1, 3)]

            def tm(s):
                r0, nr = subs[s]
                bv, bu, bz = tmb[s]
                for (bank, c0) in ((bv, C_V), (bu, C_U), (bz, C_ZG)):
                    pe_group([lambda e, k=k, bank=bank, c0=c0, r0=r0, nr=nr: e.matmul(
                        B[bank][:nr, 0:384], hTp[:, k, r0:r0 + nr], win[l][:, k, c0:c0 + 384], start=(k == 0), stop=(k == 7))
                        for k in range(8)], WK + HKS[s], [BK[bank]], 8 * 384)
                S = "%d" % s
                P.op("dve", lambda e, s=s, nr=nr, bv=bv: e.bn_stats(bst[:nr, s, :], B[bv][:nr, 0:384]), reads=[BK[bv]], writes=["bst" + S], n=384)
                P.op("dve", lambda e, s=s, nr=nr: e.bn_aggr(mv[:nr, s, :], bst[:nr, s, :]), reads=["bst" + S], writes=["mv" + S], n=8)
                rstd_op(mv[:nr, s, 1:2], rsv[:nr, s:s + 1], ["mv" + S], ["rsv" + S])
                P.op("dve", lambda e, s=s, nr=nr, bv=bv: e.tensor_scalar(nv[s][:nr, :], B[bv][:nr, 0:384], mv[:nr, s, 0:1], rsv[:nr, s:s + 1],
                                                                      ALU.subtract, ALU.mult),
                     reads=[BK[bv], "mv" + S, "rsv" + S], writes=["nv" + S], n=384)
                P.op("pool", lambda e, s=s, nr=nr: e.tensor_mul(nv[s][:nr, :], nv[s][:nr, :], gam[:nr, l, :]), reads=["nv" + S, "gam"], writes=["nv" + S], n=384)
                if is_sample:
                    P.op("pool", lambda e, s=s, nr=nr: e.tensor_add(nv[s][:nr, :], nv[s][:nr, :], bet[:nr, l, :]), reads=["nv" + S, "bet"], writes=["nv" + S], n=384)
                    P.dma("sp", "nvo", nvs[l, seq - 2], nv[s][:nr, :], reads=["nv" + S])
                    P.op("pool", lambda e, s=s, nr=nr: e.tensor_copy(vnb[s][:nr, :], nv[s][:nr, :]), reads=["nv" + S], writes=["vnb" + S], n=384)
                else:
                    P.op("pool", lambda e, s=s, nr=nr: e.tensor_add(vnb[s][:nr, :], nv[s][:nr, :], bet[:nr, l, :]), reads=["nv" + S, "bet"], writes=["vnb" + S], n=384)
                P.op("act", gas(sgug[s][:nr, :], B[bz][:nr, 0:384]), reads=[BK[bz]], writes=["sgug" + S], n=384)
                P.op("dve", lambda e, s=s, nr=nr, bu=bu: e.scalar_tensor_tensor(out=sgug[s][:nr, :], in0=B[bu][:nr, 0:384], scalar=GS, in1=sgug[s][:nr, :],
                                                                            op0=ALU.mult, op1=ALU.mult),
                     reads=[BK[bu], "sgug" + S], writes=["sgug" + S], n=384)

            def pool_chunks():
                for j in range(2):
                    bx = 4 + j
                    fm_mm(bx, 0, C_P + j * 128)
                    fm_mm(bx, 256, C_ZP + j * 128)
                for j in range(2):
                    bx = 4 + j
                    pk = "pbuf%d" % j
                    W = H + NT
                    pj = pbuf[:, j, :]
                    P.op("pool", lambda e, j=j: e.tensor_copy(pbuf[:, j, 0:H], phist[:, l, j, :]), reads=["phist%d%d" % (l, j)], writes=[pk], n=15)
                    P.op("act", lambda e, j=j, bx=bx: e.copy(pbuf[:, j, H:H + NT], B[bx][:, 0:NT]), reads=[BK[bx]], writes=[pk], n=NT)
                    P.op("act", gas(sgz[:, j, 0:NT], B[bx][:, 256:256 + NT]), reads=[BK[bx]], writes=["sgz%d" % j], n=NT)
                    P.op("pool", lambda e, j=j: e.tensor_copy(phist[:, l, j, :], pbuf[:, j, NT:NT + H]), reads=[pk], writes=["phist%d%d" % (l, j)], n=15)
                    sAj, sBj = sA[:, 0, :], sB[:, 0, :]
                    ak, bk_ = "sA", "sB"
                    P.op("pool", lambda e, pj=pj, W=W, sAj=sAj: e.tensor_add(sAj[:, 1:W], pj[:, 1:W], pj[:, 0:W - 1]), reads=[pk], writes=[ak], n=W)
                    if j == 0:
                        P.op("pool", lambda e, W=W, sAj=sAj, sBj=sBj: e.tensor_add(sBj[64:128, 3:W], sAj[64:128, 3:W], sAj[64:128, 1:W - 2]),
                             reads=[ak], writes=[bk_], n=W)
                    else:
                        P.op("pool", lambda e, W=W, sAj=sAj, sBj=sBj: e.tensor_add(sBj[:, 3:W], sAj[:, 3:W], sAj[:, 1:W - 2]), reads=[ak], writes=[bk_], n=W)
                        P.op("pool", lambda e, W=W, sAj=sAj, sBj=sBj: e.tensor_add(sAj[:, 7:W], sBj[:, 7:W], sBj[:, 3:W - 4]), reads=[bk_], writes=[ak], n=W)
                        P.op("pool", lambda e, W=W, sAj=sAj, sBj=sBj: e.tensor_add(sBj[64:128, 15:W], sAj[64:128, 15:W], sAj[64:128, 7:W - 8]),
                             reads=[ak], writes=[bk_], n=W)
                    for (h0, src, skey) in ((0, sAj, ak), (64, sBj, bk_)):
                        P.op("dve", lambda e, h0=h0, src=src, j=j, pj=pj: e.scalar_tensor_tensor(
                            out=dT[h0:h0 + 64, j, 0:NT], in0=src[h0:h0 + 64, H:H + NT], scalar=invw[h0:h0 + 64, j:j + 1],
                            in1=pj[h0:h0 + 64, H:H + NT], op0=ALU.mult, op1=ALU.subtract),
                            reads=[skey, pk, "invw"], writes=["dT%d_%d" % (j, h0)], n=NT)
                        if first:
                            n1 = min(H, NT)
                            P.op("dve", lambda e, h0=h0, src=src, j=j, n1=n1: e.tensor_mul(
                                src[h0:h0 + 64, H:H + n1], src[h0:h0 + 64, H:H + n1], invcnt[h0:h0 + 64, j, 0:n1]),
                                reads=[skey, "invcnt"], writes=[skey], n=15)
                            P.op("dve", lambda e, h0=h0, src=src, j=j, n1=n1, pj=pj: e.tensor_sub(
                                dT[h0:h0 + 64, j, 0:n1], src[h0:h0 + 64, H:H + n1], pj[h0:h0 + 64, H:H + n1]),
                                reads=[skey, pk, "dT%d_%d" % (j, h0)], writes=["dT%d_%d" % (j, h0)], n=15)
                    if emit_state is not None:
                        pe_group([lambda e, j=j: e.transpose(B[4][0:H, 0:128], pbuf[:, j, NT:NT + H], identf[:])],
                                 [pk, "identf"], [BK[4]], 512)
                        P.op("dve", lambda e, j=j: e.tensor_copy(stgp[0:H, j * 128:(j + 1) * 128], B[4][0:H, 0:128]),
                             reads=[BK[4]], writes=["hxs0"], n=128)
                if emit_state is not None:
                    P.dma("sp", "stop", emit_state[0], stgp[0:H, 0:256], reads=["hxs0"])

            def conv_chunk(j, bx, by):
                fm_mm(bx, 0, C_GC + j * 128)
                fm_mm(bx, 256, C_HX + j * 128)
                fm_mm(by, 0, C_GB + j * 128)
                fm_mm(by, 256, C_ZC + j * 128)
                cp = 0
                C = "%d" % cp
                hx_, q_, ac_, sg_ = hxs[:, cp, :], qbuf[:, cp, :], acc[:, cp, :], sgc[:, cp, :]
                P.op("act", lambda e: e.copy(hx_[:, 0:NT], B[bx][:, 256:256 + NT]), reads=[BK[bx]], writes=["hxs" + C], n=NT)
                P.op("act", gas(sg_[:, 0:NT], B[by][:, 256:256 + NT]), reads=[BK[by]], writes=["sgc" + C], n=NT)
                P.op("dve", lambda e: e.tensor_mul(sg_[:, 0:NT], B[by][:, 0:NT], sg_[:, 0:NT]), reads=[BK[by], "sgc" + C], writes=["sgc" + C], n=NT)
                P.op("pool", lambda e: e.tensor_copy(q_[:, 0:2], qhist[:, l, j, :]), reads=["qhist%d%d" % (l, j)], writes=["qbuf" + C], n=2)
                P.op("dve", lambda e: e.tensor_mul(q_[:, 2:2 + NT], B[bx][:, 0:NT], hx_[:, 0:NT]),
                     reads=[BK[bx], "hxs" + C, "qbuf" + C], writes=["qbuf" + C], n=NT)
                P.op("pool", lambda e: e.tensor_copy(qhist[:, l, j, :], q_[:, NT:NT + 2]), reads=["qbuf" + C], writes=["qhist%d%d" % (l, j)], n=2)
                P.op("act", lambda e: e.activation(out=ac_[:, 0:NT], in_=q_[:, 0:NT], func=AF.Copy, scale=wc[:, l, j, 0:1]),
                     reads=["qbuf" + C, "wc"], writes=["acc" + C], n=NT)
                for tp in (1, 2):
                    P.op("dve", lambda e, tp=tp: e.scalar_tensor_tensor(
                        out=ac_[:, 0:NT], in0=q_[:, tp:tp + NT], scalar=wc[:, l, j, tp:tp + 1], in1=ac_[:, 0:NT],
                        op0=ALU.mult, op1=ALU.add), reads=["qbuf" + C, "wc", "acc" + C], writes=["acc" + C], n=NT)
                P.op("pool", lambda e: e.tensor_mul(ycTp[:, 2 + j, 0:NT], ac_[:, 0:NT], sg_[:, 0:NT]),
                     reads=["acc" + C, "sgc" + C], writes=[YK + "_%d" % (2 + j)], n=NT)
                if emit_state is not None:
                    pe_group([lambda e: e.transpose(B[4][0:2, 0:128], q_[:, NT:NT + 2], identf[:])],
                             ["qbuf" + C, "identf"], [BK[4]], 512)
                    P.op("dve", lambda e: e.tensor_copy(stgc[0:2, j * 128:(j + 1) * 128], B[4][0:2, 0:128]),
                         reads=[BK[4]], writes=["pbuf0", "pbuf1"], n=128)

            def mix(s):
                r0, nr = subs[s]
                bm = 2 + s
                S = "%d" % s
                pe_group([lambda e, h=h, nr=nr, bm=bm: e.matmul(B[bm][:nr, h * 64:(h + 1) * 64], WT[l][:nr, h, :nr],
                                                                vnb[s][:nr, h * 64:(h + 1) * 64], start=True, stop=True)
                          for h in range(6)], ["WT%d" % l, "vnb" + S], [BK[bm]], 6 * 64)
                P.op("dve", lambda e, nr=nr, bm=bm: e.tensor_add(nv[s][:nr, :], B[bm][:nr, 0:384], bsb[:nr, l, :]),
                     reads=[BK[bm], "bsb"], writes=["nv" + S], n=384)
                P.op("pool", lambda e, nr=nr: e.tensor_mul(yg[s][:nr, :], nv[s][:nr, :], sgug[s][:nr, :]),
                     reads=["nv" + S, "sgug" + S], writes=["yg" + S], n=384)

            def ytr(s, bank):
                r0, nr = subs[s]
                S = "%d" % s
                pe_group([lambda e, c=c, nr=nr: e.transpose(Bb[bank][:, c * 128:c * 128 + nr], yg[s][:nr, c * 128:(c + 1) * 128],
                                                          ident[:nr, :nr]) for c in range(3)],
                         ["yg" + S, "ident"], [BK[bank]], 384)
                P.op("act", lambda e, r0=r0, nr=nr: e.copy(ycTp[:, 5:8, r0:r0 + nr],
                                                        Bb[bank][:, 0:384].rearrange("p (c t) -> p c t", c=3)[:, :, 0:nr]),
                     reads=[BK[bank]], writes=[YK + "_g%d" % s], n=3 * nr)

            def pool_mm():
                pe_group([lambda e, j=j: e.matmul(B[4][:, j * 256:j * 256 + NT], wpbd[l][:, j, :], dT[:, j, 0:NT], start=True, stop=True)
                          for j in range(2)], ["wpbd%d" % l, "dT0_0", "dT0_64", "dT1_0", "dT1_64"], [BK[4]], 2 * NT)
                for j in range(2):
                    P.op("dve", lambda e, j=j: e.scalar_tensor_tensor(
                        out=ycTp[:, j, 0:NT], in0=B[4][:, j * 256:j * 256 + NT], scalar=psc[:, l, j:j + 1], in1=sgz[:, j, 0:NT],
                        op0=ALU.mult, op1=ALU.mult), reads=[BK[4], "sgz%d" % j, "psc"], writes=[YK + "_%d" % j], n=NT)

            tm(0)
            if len(subs) > 1:
                tm(1)
            pool_chunks()
            conv_chunk(0, 6, 7)
            mix(0)
            conv_chunk(1, 0, 1)
            pool_mm()
            if len(subs) > 1:
                mix(1)
            conv_chunk(2, 6, 7)
            if emit_state is not None:
                P.dma("sp", "stoc", emit_state[1], stgc[0:2, 0:384], reads=["pbuf0", "pbuf1"])
            ytr(0, 5)
            if len(subs) > 1:
                ytr(1, 4)

            YKA = [YK + "_%d" % c for c in range(5)] + [YK + "_g%d" % s for s in range(len(subs))]
            obanks = [(0, 1), (4, 5)]
            for s, (r0, nr) in enumerate(subs):
                xk = XK[s]
                for hf in range(2):
                    bo = obanks[s][hf]
                    korder = [0, 1, 2, 3, 5, 6, 7, 4]
                    pe_group([lambda e, k=k, i=i, bo=bo, hf=hf, r0=r0, nr=nr: e.matmul(
                        B[bo][:nr, :], ycTp[:, k, r0:r0 + nr], wout[l][:, k, hf * 512:(hf + 1) * 512], start=(i == 0), stop=(i == 7))
                        for i, k in enumerate(korder)], ["wout%d_0" % l, "wout%d_1" % l] + YKA, [BK[bo]], 8 * 512)
                    P.op("dve", lambda e, bo=bo, hf=hf, nr=nr: e.tensor_mul(B[bo][:nr, :], B[bo][:nr, :],
                                                                         gate_bc[:nr, gslot, hf * 512:(hf + 1) * 512]),
                         reads=[BK[bo], "gate_bc%d" % gslot], writes=[BK[bo]], n=512)
                    P.op("dve", lambda e, s=s, hf=hf, nr=nr, bo=bo: e.tensor_add(xb[:nr, s, hf * 512:(hf + 1) * 512],
                                                                               B[bo][:nr, :],
                                                                               xb[:nr, s, hf * 512:(hf + 1) * 512]),
                         reads=[xk, BK[bo]], writes=[xk], n=512)
                if last_layer:
                    for hf in range(2):
                        jb_ = obanks[s][hf]
                        P.op("act", lambda e, s=s, nr=nr, jb_=jb_, hf=hf: e.activation(
                            out=B[jb_][:nr, :], in_=xb[:nr, s, hf * 512:(hf + 1) * 512], func=AF.Square,
                            scale=1.0 / 32.0, accum_out=msf[:nr, 2 * s + hf:2 * s + hf + 1]),
                            reads=[xk], writes=[BK[jb_], "msf%d_%d" % (s, hf)], n=512)
                    P.op("pool", lambda e, s=s, nr=nr: e.tensor_add(rsf[:nr, s:s + 1], msf[:nr, 2 * s:2 * s + 1], msf[:nr, 2 * s + 1:2 * s + 2]),
                         reads=["msf%d_0" % s, "msf%d_1" % s], writes=["rsf%d" % s], n=1)
                    rstd_op(rsf[:nr, s:s + 1], rsf[:nr, s:s + 1], ["rsf%d" % s], ["rsf%d" % s])
                    P.op("dve", lambda e, s=s, nr=nr: e.scalar_tensor_tensor(out=xb[:nr, s, :], in0=xb[:nr, s, :],
                                                                          scalar=rsf[:nr, s:s + 1], in1=gfin[:nr, :],
                                                                          op0=ALU.mult, op1=ALU.mult),
                         reads=[xk, "rsf%d" % s, "gfin"], writes=[xk], n=1024)
                    P.dma("sp", "yout", out_ap[t0 + r0:t0 + r0 + nr, :], xb[:nr, s, :], reads=[xk], nbytes=nr * 4096)

        def gate_load(seq, l, slot, nparts=128):
            P.dma("sp", "gate%d" % slot, gate_bc[0:nparts, slot, :], gscr[l, seq:seq + 1, :].broadcast_to([nparts, D]),
                  reads=["gscr%d%d" % (l, jb) for jb in range(8, 12)], writes=["gate_bc%d" % slot], nbytes=4096 * nparts)

        def seq_begin(seq):
            for l in range(DEPTH):
                gate_load(seq, l, l)
            PH = ["phist%d%d" % (l, j) for l in range(DEPTH) for j in range(2)]
            QH = ["qhist%d%d" % (l, j) for l in range(DEPTH) for j in range(3)]
            P.op("pool", lambda e: e.memset(phist[:], 0.0), reads=PH, writes=PH, n=60)
            P.op("pool", lambda e: e.memset(qhist[:], 0.0), reads=QH, writes=QH, n=12)

        def sample_hist_load(seq, l):
            b = seq - 2
            P.dma("sp", "stip", stgp[0:15, 0:256], stp[l, b], writes=["hxs0"])
            for j in range(2):
                pe_group([lambda e, j=j: e.transpose(B[4][:, 0:15], stgp[0:15, j * 128:(j + 1) * 128], identf[0:15, 0:15])],
                         ["hxs0", "identf"], [BK[4]], 64)
                P.op("dve", lambda e, l=l, j=j: e.tensor_copy(phist[:, l, j, :], B[4][:, 0:15]),
                     reads=[BK[4]], writes=["phist%d%d" % (l, j)], n=15)
            P.dma("sp", "stic", stgc[0:2, 0:384], stc[l, b], writes=["pbuf0", "pbuf1"])
            for j in range(3):
                pe_group([lambda e, j=j: e.transpose(B[4][:, 0:2], stgc[0:2, j * 128:(j + 1) * 128], identf[0:2, 0:2])],
                         ["pbuf0", "pbuf1", "identf"], [BK[4]], 64)
                P.op("dve", lambda e, l=l, j=j: e.tensor_copy(qhist[:, l, j, :], B[4][:, 0:2]),
                     reads=[BK[4]], writes=["qhist%d%d" % (l, j)], n=2)

        import os as _os
        KSTOP = int(_os.environ.get("KSTOP", "100000"))
        _tl = tile_layer
        _cnt = [0]

        def tile_layer(*a):
            _cnt[0] += 1
            if _cnt[0] <= KSTOP:
                _tl(*a)
        ntile = SEQ // TT
        gt = 0
        for seq in range(2):
            seq_begin(seq)
            for t2 in range(0, ntile, 2):
                tl = [t for t in (t2, t2 + 1) if t < ntile]
                pps = {}
                for t in tl:
                    pp = gt % NXB
                    gt += 1
                    pps[t] = pp
                    P.dma("sp", "xin%d" % pp, xbuf[pp][:], xp[seq, t * TT:(t + 1) * TT, :].rearrange("(s p) d -> p s d", p=128),
                          writes=["x%d_0" % pp, "x%d_1" % pp], nbytes=1048576)
                for l in range(DEPTH):
                    for t in tl:
                        es = None
                        if t == ntile - 1:
                            es = (npp[l, seq], ncp[l, seq])
                        tile_layer(seq, l, TT, [(0, 128), (128, 128)], t == 0, es, False, l == DEPTH - 1, y_p[seq], t * TT, pps[t])
        spp = {}
        for seq in (2, 3):
            pp = gt % NXB
            gt += 1
            spp[seq] = pp
            P.dma("sp", "xin%d" % pp, xbuf[pp][0:NS, 0, :], xs[seq - 2], writes=["x%d_0" % pp, "x%d_1" % pp])
        for l in range(DEPTH):
            for seq in (2, 3):
                sample_hist_load(seq, l)
                gate_load(seq, l, seq - 2, nparts=NS)
                es = (nps[l, seq - 2], ncs[l, seq - 2])
                tile_layer(seq, l, NS, [(0, NS)], False, es, True, l == DEPTH - 1, y_s[seq - 2], 0, spp[seq], seq - 2)

        P.emit(reorder=REORDER)
    return nc


def make_in_maps(inputs, n_cores=8):
    g = lambda k: np.ascontiguousarray(np.asarray(inputs[k], dtype=np.float32))
    xp, xs, stp, stc = g("x_prompt"), g("x_sample"), g("state_pool"), g("state_conv")
    cp, cs = g("c_prompt"), g("c_sample")
    shared = {k: g(k) for k in ("norm_g", "w_ada", "b_ada", "w_in", "w_pool", "pool_scale", "w_conv",
                                "v_norm_g", "v_norm_b", "w_s", "b_s", "w_out")}
    shared["final_norm_g"] = g("final_norm_g").reshape(1, -1)
    maps = []
    for c in range(n_cores):
        sl = slice(2 * c, 2 * c + 2)
        m = dict(shared)
        m["xp"] = np.ascontiguousarray(xp[sl])
        m["xs"] = np.ascontiguousarray(xs[sl])
        m["stp"] = np.ascontiguousarray(stp[:, sl])
        m["stc"] = np.ascontiguousarray(stc[:, sl])
        m["c4"] = np.ascontiguousarray(np.concatenate([cp[sl], cs[sl]], axis=0))
        maps.append(m)
    return maps


def gather(results):
    cat = lambda k, ax: np.concatenate([np.asarray(r[k], dtype=np.float32) for r in results], axis=ax)
    return (cat("y_p", 0), cat("y_s", 0), cat("npp", 1), cat("ncp", 1), cat("nps", 1), cat("ncs", 1), cat("nvs", 1))


def kernel(**inputs):
    n = 8
    seq = inputs["x_prompt"].shape[1]
    ns = inputs["x_sample"].shape[1]
    nc = build_nc(seq, ns)
    res = run_bass_kernel_spmd(nc, make_in_maps(inputs, n), core_ids=list(range(n)))
    return gather(res.results)
```

```python
import contextlib
import numpy as np
import concourse.bass as bass
import concourse.mybir as mybir
from concourse.bass_utils import run_bass_kernel_spmd

F32 = mybir.dt.float32
BF16 = mybir.dt.bfloat16
ALU = mybir.AluOpType
AF = mybir.ActivationFunctionType

D = 1024
DEPTH = 2
PW = 3200
EPS = 1e-6
GS = 1.702
TT = 256
NXB = 2
NHT = 1
EXPERIMENT_SHRINK = False
REORDER = True
PRIO_CP = True
PRIO_BUCKET = 25
HOP_NS = 400.0
C_P, C_ZP, C_GB, C_GC, C_HX, C_ZC, C_U, C_V, C_ZG = 0, 256, 512, 896, 1280, 1664, 2048, 2432, 2816


class Op:
    __slots__ = ("idx", "eng", "fn", "deps", "order", "inc", "dur", "lat", "dsem", "seq", "dcount", "done")

    def __init__(self, idx, eng, fn, deps, inc, dur, lat=0.0, dsem=None):
        self.idx, self.eng, self.fn, self.deps, self.inc = idx, eng, fn, deps, inc
        self.order = []
        self.dur, self.lat, self.dsem = dur, lat, dsem
        self.seq = 0
        self.dcount = 0
        self.done = 0.0


class Prog:
    ENGS = ("pe", "act", "dve", "pool", "sp")

    def __init__(self, nc, stack):
        self.nc = nc
        self.stack = stack
        self.all = []
        self.sem = {e: stack.enter_context(nc.semaphore("c_" + e)) for e in self.ENGS}
        self.buf = {}
        self.dsem = {}
        self.dlast = {}
        self.groups = {}

    def _deps(self, eng, reads, writes):
        deps = []
        for k in reads:
            st = self.buf.get(k)
            if st and st[0] is not None:
                deps.append(st[0])
        for k in writes:
            st = self.buf.get(k)
            if st:
                if st[0] is not None:
                    deps.append(st[0])
                deps.extend(st[1].values())
        return deps

    def _mark(self, me, reads, writes):
        for k in reads:
            st = self.buf.setdefault(k, [None, {}])
            st[1][id(me)] = me
        for k in writes:
            self.buf[k] = [me, {}]

    def op(self, eng, fn, reads=(), writes=(), n=256, dur=None):
        if dur is None:
            dur = {"act": (130 + 0.55 * n) if n <= 512 else (200 + 1.0 * n), "dve": 200 + 0.85 * n,
                   "pool": 180 + 1.8 * n, "pe": 60 + n / 2.0}[eng]
        deps = self._deps(eng, reads, writes)
        o = Op(len(self.all), eng, fn, deps, True, dur)
        self.all.append(o)
        self._mark(o, reads, writes)
        return o

    def dma(self, q, name, out, in_, reads=(), writes=(), batch=None, nbytes=65536, **kw):
        if name not in self.dsem:
            self.dsem[name] = self.stack.enter_context(self.nc.semaphore("d_" + name))
        deps = self._deps(q, reads, writes)

        def fn(e, out=out, in_=in_, kw=kw):
            return e.dma_start(out=out, in_=in_, **kw)
        o = Op(len(self.all), q, fn, deps, False, 150.0 if q == "sp" else 1200.0, 2000.0 + nbytes / 250.0, name)
        if batch is not None:
            g = self.groups.setdefault((name, batch), [])
            o.deps = [d for d in deps if d not in g]
            g.append(o)
        prev = self.dlast.get(name)
        if prev is not None:
            o.order.append(prev)
        self.dlast[name] = o
        self.all.append(o)
        self._mark(o, reads, writes)
        return o

    def schedule(self):
        import heapq
        ops = self.all
        member = {}
        for g in self.groups.values():
            for o in g:
                member[id(o)] = g
        for o in ops:
            extra = []
            for d in o.deps:
                g = member.get(id(d))
                if g is not None and o not in g:
                    extra.extend(g)
            if extra:
                o.deps = list(o.deps) + extra
        nsucc = [[] for _ in ops]
        indeg = [0] * len(ops)
        for o in ops:
            ds = set(id(d) for d in o.deps) | set(id(d) for d in o.order)
            preds = {d.idx: d for d in list(o.deps) + list(o.order)}
            indeg[o.idx] = len(preds)
            for d in preds.values():
                nsucc[d.idx].append(o)
        heaps = {e: [] for e in self.ENGS}
        prio = [0.0] * len(ops)
        if PRIO_CP:
            for o in reversed(ops):
                best = 0.0
                for s_ in nsucc[o.idx]:
                    if prio[s_.idx] > best:
                        best = prio[s_.idx]
                prio[o.idx] = best + o.dur + (o.lat if o.dsem else 120.0)
        ready_t = [0.0] * len(ops)
        for o in ops:
            if indeg[o.idx] == 0:
                heapq.heappush(heaps[o.eng], (0.0, o.idx))
        t_eng = {e: 0.0 for e in self.ENGS}
        self.gorder = []
        dma_free = 0.0
        order = {e: [] for e in self.ENGS}
        nleft = len(ops)
        while nleft:
            best = None
            for e in self.ENGS:
                h = heaps[e]
                if not h:
                    continue
                rt, idx = h[0]
                if rt <= t_eng[e]:
                    cand = []
                    while h and h[0][0] <= t_eng[e]:
                        cand.append(heapq.heappop(h))
                    if PRIO_CP:
                        pick = min(cand, key=lambda c: (c[1] // PRIO_BUCKET, -prio[c[1]], c[1]))
                    else:
                        pick = min(cand, key=lambda c: c[1])
                    for c in cand:
                        if c is not pick:
                            heapq.heappush(h, c)
                    start = t_eng[e]
                else:
                    pick = heapq.heappop(h)
                    start = rt
                if best is None or start < best[0]:
                    if best is not None:
                        heapq.heappush(heaps[best[1]], best[2])
                    best = (start, e, pick)
                else:
                    heapq.heappush(h, pick)
            start, e, pick = best
            o = ops[pick[1]]
            self.gorder.append(o)
            t_eng[e] = start + o.dur
            if o.dsem:
                xfer = (o.lat - 2000.0) * 250.0 / 270.0
                dma_free = max(dma_free, start) + xfer
                o.done = max(start + 2000.0, dma_free + 1000.0)
            else:
                o.done = start + o.dur + HOP_NS
            order[e].append(o)
            nleft -= 1
            for s in nsucc[o.idx]:
                ready_t[s.idx] = max(ready_t[s.idx], o.done)
                indeg[s.idx] -= 1
                if indeg[s.idx] == 0:
                    heapq.heappush(heaps[s.eng], (ready_t[s.idx], s.idx))
        self.est_ns = max(t_eng.values())
        return order

    def emit(self, final_sems=("yout", "stop", "stoc", "nvo"), reorder=True):
        nc = self.nc
        if reorder:
            order = self.schedule()
        else:
            order = {e: [o for o in self.all if o.eng == e] for e in self.ENGS}
        dcnt = {}
        for e in self.ENGS:
            c = 0
            for o in order[e]:
                if o.dsem:
                    dcnt[o.dsem] = dcnt.get(o.dsem, 0) + 16
                    o.dcount = dcnt[o.dsem]
                else:
                    c += 1
                    o.seq = c
        for g in self.groups.values():
            m = max(o.dcount for o in g)
            for o in g:
                o.dcount = m
        gorder = self.gorder if reorder else list(self.all)
        know = {e: {} for e in self.ENGS}
        clock = {}
        waits = {}
        for o in gorder:
            e = o.eng
            K = know[e]
            needs = []
            for d in o.deps:
                if d.dsem:
                    needs.append((("d", d.dsem), self.dsem[d.dsem], d.dcount, d))
                else:
                    if d.eng == e and e == "pe":
                        continue
                    needs.append((("e", d.eng), self.sem[d.eng], d.seq, d))
            w = []
            for key, sobj, v, d in sorted(needs, key=lambda t: -t[2]):
                if K.get(key, 0) < v:
                    w.append((sobj, v))
                    K[key] = v
                    for k2, v2 in clock[id(d)].items():
                        if K.get(k2, 0) < v2:
                            K[k2] = v2
            waits[id(o)] = w
            c = dict(K)
            if o.dsem:
                if c.get(("d", o.dsem), 0) < o.dcount:
                    c[("d", o.dsem)] = o.dcount
            else:
                c[("e", e)] = max(c.get(("e", e), 0), o.seq)
            clock[id(o)] = c
        self.n_waits = sum(len(w) for w in waits.values())
        handles = {"pe": "tensor", "act": "scalar", "dve": "vector", "pool": "gpsimd", "sp": "sync"}
        with nc.Block() as block:
            for ename in self.ENGS:
                ops = order[ename]
                if not ops:
                    continue

                def body(e, ops=ops, ename=ename):
                    own = self.sem[ename]
                    for o in ops:
                        for sobj, v in waits[id(o)]:
                            e.wait_ge(sobj, v)
                        ins = o.fn(e)
                        if o.dsem:
                            ins.then_inc(self.dsem[o.dsem], 16)
                        else:
                            ins.then_inc(own, 1)
                    if ename == "sp":
                        for n in final_sems:
                            if n in dcnt:
                                e.wait_ge(self.dsem[n], dcnt[n])
                getattr(block, handles[ename])(body)


def build_nc(SEQ=4096, NS=16):
    nc = bass.Bass("TRN2", target_bir_lowering=False)
    dt_in = lambda n, s: nc.dram_tensor(n, s, F32, kind="ExternalInput").ap()
    dt_out = lambda n, s: nc.dram_tensor(n, s, F32, kind="ExternalOutput").ap()
    xp = dt_in("xp", [2, SEQ, D])
    xs = dt_in("xs", [2, NS, D])
    stp = dt_in("stp", [DEPTH, 2, 15, 256])
    stc = dt_in("stc", [DEPTH, 2, 2, 384])
    c4 = dt_in("c4", [4, D])
    norm_g = dt_in("norm_g", [DEPTH, D])
    w_ada = dt_in("w_ada", [DEPTH, D, 3 * D])
    b_ada = dt_in("b_ada", [DEPTH, 3 * D])
    w_in = dt_in("w_in", [DEPTH, D, PW])
    w_pool = dt_in("w_pool", [DEPTH, 4, 64, 64])
    pool_scale = dt_in("pool_scale", [DEPTH, 256])
    w_conv = dt_in("w_conv", [DEPTH, 3, 384])
    v_norm_g = dt_in("v_norm_g", [DEPTH, 384])
    v_norm_b = dt_in("v_norm_b", [DEPTH, 384])
    w_s = dt_in("w_s", [DEPTH, 6, 128, 128])
    b_s = dt_in("b_s", [DEPTH, 6, 128])
    w_out = dt_in("w_out", [DEPTH, D, D])
    fng = dt_in("final_norm_g", [1, D])
    y_p = dt_out("y_p", [2, SEQ, D])
    y_s = dt_out("y_s", [2, NS, D])
    npp = dt_out("npp", [DEPTH, 2, 15, 256])
    ncp = dt_out("ncp", [DEPTH, 2, 2, 384])
    nps = dt_out("nps", [DEPTH, 2, 15, 256])
    ncs = dt_out("ncs", [DEPTH, 2, 2, 384])
    nvs = dt_out("nvs", [DEPTH, 2, NS, 384])
    gscr = nc.dram_tensor("gscr", [DEPTH, 4, D], F32, kind="Internal").ap()

    with contextlib.ExitStack() as st:
        P = Prog(nc, st)
        sb = lambda n, s, d=F32: st.enter_context(nc.sbuf_tensor(n, s, d))
        nc_ncd = st.enter_context(nc.allow_non_contiguous_dma(reason="tiny constant layouts"))
        B = [st.enter_context(nc.psum_tensor("B%d" % i, [128, 512], F32)) for i in range(8)]
        Bb = [b[:].bitcast(BF16) for b in B]
        BK = ["B%d" % i for i in range(8)]

        win = [sb("win%d" % l, [128, 8, PW], BF16) for l in range(DEPTH)]
        wout = [sb("wout%d" % l, [128, 8, D], BF16) for l in range(DEPTH)]
        xbuf = [sb("xbuf%d" % i, [128, 2, D]) for i in range(NXB)]
        xn = sb("xn", [128, 2, D], BF16)
        otmp = xn[:].rearrange("p s d -> p (s d)").bitcast(F32)
        hT = [sb("hT%d" % i, [128, 8, TT], BF16) for i in range(NHT)]
        ycT = [sb("ycT0", [128, 8, TT], BF16)]
        gate_bc = sb("gate_bc", [128, DEPTH, D])
        gfin = sb("gfin", [128, D]) if not EXPERIMENT_SHRINK else gate_bc[:, 0, :]
        pbuf = sb("pbuf", [128, 2, 15 + TT])
        sA = sb("sA", [128, 1, 15 + TT])
        sB = sb("sB", [128, 1, 15 + TT])
        dT = sb("dT", [128, 2, TT], BF16)
        sgz = sb("sgz", [128, 2, TT])
        phist = sb("phist", [128, DEPTH, 2, 15])
        hxs = sb("hxs", [128, 1, TT])
        qbuf = sb("qbuf", [128, 1, 2 + TT])
        acc = sb("acc", [128, 1, TT])
        sgc = sb("sgc", [128, 1, TT])
        qhist = sb("qhist", [128, DEPTH, 3, 2])
        sgug = [sb("sgug%d" % i, [128, 384]) for i in range(2)]
        nv = [sb("nv%d" % i, [128, 384]) for i in range(2)]
        vnb = [sb("vnb%d" % i, [128, 384], BF16) for i in range(2)]
        yg = [sb("yg%d" % i, [128, 384], BF16) for i in range(2)]
        bst = sb("bst", [128, 2, 6])
        mv = sb("mv", [128, 2, 2])
        rsv = sb("rsv", [128, 2])
        ms = sb("ms", [128, 2])
        msf = sb("msf", [128, 4])
        rsf = sb("rsf", [128, 2])
        rs = sb("rs", [128, 2])
        ident = sb("ident", [128, 128], BF16)
        identf = sb("identf", [128, 128])
        nhalf = sb("nhalf", [128, 1])
        barr = sb("barr", [128, 1])
        WT = [sb("WT%d" % l, [128, 6, 128], BF16) for l in range(DEPTH)]
        wstg = xbuf[0][:, 1, 0:768].rearrange("p (h s) -> p h s", h=6)
        wpbd = [sb("wpbd%d" % l, [128, 2, 128], BF16) for l in range(DEPTH)]
        gam = sb("gam", [128, DEPTH, 384])
        bet = sb("bet", [128, DEPTH, 384])
        bsb = sb("bsb", [128, DEPTH, 384])
        bs6 = sb("bs6", [128, DEPTH, 6])
        psc = sb("psc", [128, DEPTH, 2])
        wc = sb("wc", [128, DEPTH, 3, 3])
        ngT = sb("ngT", [128, DEPTH, 8])
        invw = sb("invw", [128, 2])
        invcnt = sb("invcnt", [128, 2, 15])
        gsT = sb("gsT", [128, DEPTH, 8, 4])
        shT = sb("shT", [128, DEPTH, 8, 4])
        c4t = xbuf[0][0:4, 0, :]
        scT = sb("scT", [128, 8, 4], BF16)
        mrow = otmp[0:4, 0:512].rearrange("p (a b) -> p a b", a=2)
        bblk = otmp[0:4, 512:1024].rearrange("p (a b) -> p a b", a=2)
        stgp = hxs[0:16, 0, :]
        stgc = pbuf[0:2, :, :].rearrange("p a b -> p (a b)")[:, 0:384]

        def pe_group(fns, reads, writes, n):
            def fn(e, fns=fns):
                ins = None
                for f in fns:
                    ins = f(e)
                return ins
            return P.op("pe", fn, reads=reads, writes=writes, dur=8.0 * len(fns) + n / 2.3)

        P.op("pool", lambda e: e.memset(identf[:], 0.0), writes=["identf"])
        P.op("pool", lambda e: e.affine_select(out=identf[:], in_=identf[:], pattern=[[-1, 128]],
                                               compare_op=ALU.not_equal, fill=1.0, base=0,
                                               channel_multiplier=1),
             reads=["identf"], writes=["identf"])
        P.op("dve", lambda e: e.tensor_copy(ident[:], identf[:]), reads=["identf"], writes=["ident"])
        P.op("pool", lambda e: e.memset(nhalf[:], -0.5), writes=["nhalf"])
        wins = (2, 4, 8, 16)
        for g, w in enumerate(wins):
            j, h0 = g // 2, (g % 2) * 64
            P.op("pool", lambda e, j=j, h0=h0, w=w: e.memset(invw[h0:h0 + 64, j:j + 1], 1.0 / w), writes=["invw"], n=1)
            P.op("pool", lambda e, j=j, h0=h0, w=w: e.memset(invcnt[h0:h0 + 64, j, :], 1.0 / w), writes=["invcnt"], n=15)
            for pos in range(w - 1):
                P.op("pool", lambda e, j=j, h0=h0, pos=pos: e.memset(invcnt[h0:h0 + 64, j, pos:pos + 1], 1.0 / (pos + 1)),
                     writes=["invcnt"], n=1)
        P.dma("sp", "cst", c4t[:], c4, writes=["x0_0"], batch=0)
        P.dma("sp", "cst", gfin[:], fng.broadcast_to([128, D]), writes=["gfin"], batch=0)
        for l in range(DEPTH):
            P.dma("sp", "cst", gam[:, l, :], v_norm_g[l:l + 1, :].broadcast_to([128, 384]), writes=["gam"], batch=0)
            P.dma("sp", "cst", bet[:, l, :], v_norm_b[l:l + 1, :].broadcast_to([128, 384]), writes=["bet"], batch=0)
            P.dma("sp", "cst", bs6[:, l, :], b_s[l].rearrange("h t -> t h"), writes=["bs6"], batch=0)
            P.dma("sp", "cst", psc[:, l, :], pool_scale[l].rearrange("(j p) -> p j", p=128), writes=["psc"], batch=0)
            for j in range(3):
                P.dma("sp", "cst", wc[:, l, j, :], w_conv[l][:, j * 128:(j + 1) * 128].rearrange("k p -> p k"), writes=["wc"], batch=0)
            P.dma("sp", "cst", ngT[:, l, :], norm_g[l].rearrange("(k p) -> p k", p=128), writes=["ngT"], batch=0)
        P.op("dve", lambda e: e.tensor_scalar_mul(psc[:], psc[:], GS), reads=["psc"], writes=["psc"], n=4)
        P.op("dve", lambda e: e.tensor_scalar_mul(wc[:], wc[:], GS), reads=["wc"], writes=["wc"], n=18)
        for l in range(DEPTH):
            P.op("dve", lambda e, l=l: e.tensor_copy(bsb[:, l, :].rearrange("p (h c) -> p h c", h=6),
                                                      bs6[:, l, :].unsqueeze(2).to_broadcast([128, 6, 64])),
                 reads=["bs6"], writes=["bsb"], n=384)
        P.op("act", lambda e: e.activation(out=c4t[:], in_=c4t[:], func=AF.Gelu_apprx_sigmoid, scale=1.0 / GS),
             reads=["x0_0"], writes=["x0_0"], n=1024)
        pe_group([lambda e, k=k: e.transpose(B[0][:, k * 4:(k + 1) * 4], c4t[0:4, k * 128:(k + 1) * 128], identf[0:4, 0:4])
                  for k in range(8)], ["x0_0", "identf"], [BK[0]], 128)
        P.op("dve", lambda e: e.tensor_copy(scT[:].rearrange("p k s -> p (k s)"), B[0][:, 0:32]), reads=[BK[0]], writes=["scT"], n=32)
        blk = 0
        for l in range(DEPTH):
            for jb in range(12):
                par = blk % 2
                HT0K = ["hT0_%d_%d" % (s_, k_) for s_ in range(2) for k_ in range(8)]
                YC0K = ["ycT0_%d" % c_ for c_ in range(5)] + ["ycT0_g0", "ycT0_g1"]
                x1v = xbuf[1][:].rearrange("p s d -> p (s d)").bitcast(BF16)
                stages = [(hT[0], HT0K), (ycT[0], YC0K),
                          (x1v[:, 0:2048].rearrange("p (k n) -> p k n", k=8), ["x1_0"]),
                          (x1v[:, 2048:4096].rearrange("p (k n) -> p k n", k=8), ["x1_1"])]
                sbuf_stage, skeys = stages[blk % 4]
                spar = blk % 4
                bx, by = 2 + par, 4 + par
                c0 = jb * 256
                last_wada = P.dma("pool", "wada%d" % spar, sbuf_stage[:], w_ada[l][:, c0:c0 + 256].rearrange("(k p) n -> p k n", p=128),
                                  writes=skeys, nbytes=1048576)
                P.dma("sp", "bblk%d" % par, bblk[:, par, :], b_ada[l:l + 1, c0:c0 + 256].broadcast_to([4, 256]),
                      writes=["bblk%d" % par])
                pe_group([lambda e, k=k, bx=bx, s_=sbuf_stage: e.matmul(B[bx][0:4, 0:256], scT[:, k, :], s_[:, k, :],
                                                                       start=(k == 0), stop=(k == 7)) for k in range(8)],
                         ["scT"] + skeys, [BK[bx]], 8 * 256)
                P.op("dve", lambda e, bx=bx, par=par: e.scalar_tensor_tensor(out=mrow[:, par, :], in0=B[bx][0:4, 0:256], scalar=GS,
                                                                          in1=bblk[:, par, :], op0=ALU.mult, op1=ALU.add),
                     reads=[BK[bx], "bblk%d" % par], writes=["mrow%d" % par])
                if jb < 8:
                    pe_group([lambda e, i=i, by=by, par=par: e.transpose(B[by][:, i * 4:(i + 1) * 4],
                                                                      mrow[0:4, par, i * 128:(i + 1) * 128], identf[0:4, 0:4])
                              for i in range(2)], ["mrow%d" % par, "identf"], [BK[by]], 64)
                    if jb < 4:
                        P.op("dve", lambda e, l=l, jb=jb, by=by: e.tensor_copy(
                            shT[:, l, 2 * jb:2 * jb + 2, :], B[by][:, 0:8].rearrange("p (i s) -> p i s", i=2)),
                            reads=[BK[by]], writes=["shT"], n=8)
                    else:
                        for i in range(2):
                            kk = 2 * (jb - 4) + i
                            P.op("dve", lambda e, l=l, kk=kk, i=i, by=by: e.scalar_tensor_tensor(
                                out=gsT[:, l, kk, :], in0=B[by][:, i * 4:(i + 1) * 4], scalar=1.0,
                                in1=ngT[:, l, kk:kk + 1].to_broadcast([128, 4]), op0=ALU.add, op1=ALU.mult),
                                reads=[BK[by], "ngT"], writes=["gsT"], n=4)
                else:
                    P.dma("sp", "gscr%d" % par, gscr[l, :, (jb - 8) * 256:(jb - 7) * 256], mrow[:, par, :],
                          reads=["mrow%d" % par], writes=["gscr%d%d" % (l, jb)])
                blk += 1

        for l in range(DEPTH):
            P.dma("sp", "x0_1", wstg[:], w_s[l].rearrange("h t s -> t h s"), reads=[], writes=["x0_1"], nbytes=393216)
            for h in range(6):
                bk = 2 + (h % 2)
                pe_group([lambda e, h=h, bk=bk: e.transpose(B[bk][:, 0:128], wstg[:, h, :], identf[:])],
                         ["x0_1", "identf"], [BK[bk]], 512)
                P.op("dve", lambda e, l=l, h=h, bk=bk: e.tensor_copy(WT[l][:, h, :], B[bk][:, 0:128]),
                     reads=[BK[bk]], writes=["WT%d" % l], n=128)
            P.op("pool", lambda e, l=l: e.memset(WT[l][64:128, :, 0:64], 0.0), reads=["WT%d" % l], writes=["WT%d" % l], n=384)
            P.op("pool", lambda e: e.memset(wstg[:, 0:2, :], 0.0), writes=["x0_1"], n=256)
            for g in range(4):
                j, h0 = g // 2, (g % 2) * 64
                P.dma("sp", "x0_1", wstg[h0:h0 + 64, j, h0:h0 + 64], w_pool[l, g], reads=["x0_1"], writes=["wstgd%d" % g])
            P.op("dve", lambda e, l=l: e.tensor_copy(wpbd[l][:], wstg[:, 0:2, :]),
                 reads=["x0_1", "wstgd0", "wstgd1", "wstgd2", "wstgd3"], writes=["wpbd%d" % l], n=256)

        P.op("pool", lambda e: e.memset(barr[:], 0.0), writes=["mrow0", "mrow1", "bblk0", "bblk1", "xn0", "xn1"], n=1)

        prev = last_wada
        for l in range(DEPTH):
            for k in range(8):
                o = P.dma("pool", "win%d" % l, win[l][:, k, :], w_in[l][k * 128:(k + 1) * 128, :], writes=["win%d_%d" % (l, k)],
                          batch=0, nbytes=1638400)
                o.order.append(prev)
                prev = o
            for k in range(0, 8, 4):
                o = P.dma("pool", "wout%d" % l, wout[l][:, k:k + 4, :],
                          w_out[l][k * 128:(k + 4) * 128, :].rearrange("(k p) n -> p k n", p=128), writes=["wout%d_%d" % (l, k // 4)],
                          batch=0, nbytes=2097152)
                o.order.append(prev)
                prev = o

        def rstd_op(src, dst, reads, writes):
            P.op("pool", lambda e: e.tensor_scalar_add(dst, src, EPS), reads=reads, writes=writes, n=1)
            P.op("pool", lambda e: e.tensor_tensor(dst, dst, nhalf[:dst.shape[0], :], ALU.pow),
                 reads=writes + ["nhalf"], writes=writes, dur=490.0)

        def gas(out, in_):
            return lambda e: e.activation(out=out, in_=in_, func=AF.Gelu_apprx_sigmoid, scale=1.0 / GS)

        import os as _os2
        KPH = int(_os2.environ.get("KPH", "100"))
        KQ = int(_os2.environ.get("KQ", "100"))

        def tile_layer(seq, l, NT, subs, first, emit_state, is_sample, last_layer, out_ap, t0, pp, gslot=None):
            if gslot is None:
                gslot = l
            H = 15
            xb, hTp, ycTp = xbuf[pp], hT[pp % NHT], ycT[0]
            XK = ["x%d_%d" % (pp, s) for s in range(2)]
            HK, YK = "hT%d" % (pp % NHT), "ycT0"
            WK = ["win%d_%d" % (l, k) for k in range(8)]

            def fm_mm(bank, off, c0):
                pe_group([lambda e, k=k: e.matmul(B[bank][:, off:off + NT], win[l][:, k, c0:c0 + 128], hTp[:, k, 0:NT],
                                                  start=(k == 0), stop=(k == 7)) for k in range(8)],
                         WK + [HK], [BK[bank]], 8 * NT)

            for s, (r0, nr) in enumerate(subs):
                xk = XK[s]
                P.op("act", lambda e, s=s, nr=nr: e.activation(out=xn[:nr, s, :], in_=xb[:nr, s, :], func=AF.Square,
                                                             scale=1.0 / 32.0, accum_out=ms[:nr, s:s + 1]),
                     reads=[xk], writes=["xn%d" % s, "ms%d" % s], n=1024)
                if KQ <= 0:
                    continue
                rstd_op(ms[:nr, s:s + 1], rs[:nr, s:s + 1], ["ms%d" % s], ["rs%d" % s])
                if KQ <= 1:
                    continue
                P.op("act", lambda e, s=s, nr=nr: e.activation(out=xn[:nr, s, :], in_=xb[:nr, s, :], func=AF.Copy,
                                                             scale=rs[:nr, s:s + 1]),
                     reads=[xk, "rs%d" % s], writes=["xn%d" % s], n=1024)
                if KQ <= 2:
                    continue
                tbank = (2 + s, 6 + s)
                for hb in range(2):
                    pe_group([lambda e, s=s, nr=nr, k=k, tb=tbank[hb]: e.transpose(Bb[tb][:, (k % 4) * 128:(k % 4) * 128 + nr],
                                                                            xn[:nr, s, k * 128:(k + 1) * 128], ident[:nr, :nr])
                              for k in range(4 * hb, 4 * hb + 4)], ["xn%d" % s, "ident"], [BK[tbank[hb]]], 4 * 128)
                if KQ <= 3:
                    continue
                for k in range(8):
                    hb = k // 4
                    o = hTp[:, k, r0:r0 + nr]
                    i_ = Bb[tbank[hb]][:, (k % 4) * 128:(k % 4) * 128 + nr]
                    g_ = gsT[:, l, k, seq:seq + 1]
                    h_ = shT[:, l, k, seq:seq + 1]
                    if hb == 0:
                        P.op("act", lambda e, o=o, i_=i_, g_=g_, h_=h_: e.activation(out=o, in_=i_, func=AF.Identity, bias=h_, scale=g_),
                             reads=[BK[tbank[hb]], "gsT", "shT"], writes=[HK + "_%d_%d" % (s, k)], n=nr)
                    else:
                        P.op("dve", lambda e, o=o, i_=i_, g_=g_, h_=h_: e.tensor_scalar(o, i_, g_, h_, ALU.mult, ALU.add),
                             reads=[BK[tbank[hb]], "gsT", "shT"], writes=[HK + "_%d_%d" % (s, k)], n=nr)
            HKS = [[HK + "_%d_%d" % (s, k) for k in range(8)] for s in range(len(subs))]
            HKA = [k_ for ks in HKS for k_ in ks]

            def fm_mm(bank, off, c0):
                pe_group([lambda e, k=k: e.matmul(B[bank][:, off:off + NT], win[l][:, k, c0:c0 + 128], hTp[:, k, 0:NT],
                                                  start=(k == 0), stop=(k == 7)) for k in range(8)],
                         WK + HKA, [BK[bank]], 8 * NT)

            tmb = [(6, 7, 2), (0, 1, 3)]

            def tm(s):
                r0, nr = subs[s]
                bv, bu, bz = tmb[s]
                for (bank, c0) in ((bv, C_V), (bu, C_U), (bz, C_ZG)):
                    pe_group([lambda e, k=k, bank=bank, c0=c0, r0=r0, nr=nr: e.matmul(
                        B[bank][:nr, 0:384], hTp[:, k, r0:r0 + nr], win[l][:, k, c0:c0 + 384], start=(k == 0), stop=(k == 7))
                        for k in range(8)], WK + HKS[s], [BK[bank]], 8 * 384)
                S = "%d" % s
                P.op("dve", lambda e, s=s, nr=nr, bv=bv: e.bn_stats(bst[:nr, s, :], B[bv][:nr, 0:384]), reads=[BK[bv]], writes=["bst" + S], n=384)
                P.op("dve", lambda e, s=s, nr=nr: e.bn_aggr(mv[:nr, s, :], bst[:nr, s, :]), reads=["bst" + S], writes=["mv" + S], n=8)
                rstd_op(mv[:nr, s, 1:2], rsv[:nr, s:s + 1], ["mv" + S], ["rsv" + S])
                P.op("dve", lambda e, s=s, nr=nr, bv=bv: e.tensor_scalar(nv[s][:nr, :], B[bv][:nr, 0:384], mv[:nr, s, 0:1], rsv[:nr, s:s + 1],
                                                                      ALU.subtract, ALU.mult),
                     reads=[BK[bv], "mv" + S, "rsv" + S], writes=["nv" + S], n=384)
                P.op("pool", lambda e, s=s, nr=nr: e.tensor_mul(nv[s][:nr, :], nv[s][:nr, :], gam[:nr, l, :]), reads=["nv" + S, "gam"], writes=["nv" + S], n=384)
                if is_sample:
                    P.op("pool", lambda e, s=s, nr=nr: e.tensor_add(nv[s][:nr, :], nv[s][:nr, :], bet[:nr, l, :]), reads=["nv" + S, "bet"], writes=["nv" + S], n=384)
                    P.dma("sp", "nvo", nvs[l, seq - 2], nv[s][:nr, :], reads=["nv" + S])
                    P.op("pool", lambda e, s=s, nr=nr: e.tensor_copy(vnb[s][:nr, :], nv[s][:nr, :]), reads=["nv" + S], writes=["vnb" + S], n=384)
                else:
                    P.op("pool", lambda e, s=s, nr=nr: e.tensor_add(vnb[s][:nr, :], nv[s][:nr, :], bet[:nr, l, :]), reads=["nv" + S, "bet"], writes=["vnb" + S], n=384)
                P.op("act", gas(sgug[s][:nr, :], B[bz][:nr, 0:384]), reads=[BK[bz]], writes=["sgug" + S], n=384)
                P.op("dve", lambda e, s=s, nr=nr, bu=bu: e.scalar_tensor_tensor(out=sgug[s][:nr, :], in0=B[bu][:nr, 0:384], scalar=GS, in1=sgug[s][:nr, :],
                                                                            op0=ALU.mult, op1=ALU.mult),
                     reads=[BK[bu], "sgug" + S], writes=["sgug" + S], n=384)

            def pool_chunks():
                for j in range(2):
                    bx = 4 + j
                    fm_mm(bx, 0, C_P + j * 128)
                    fm_mm(bx, 256, C_ZP + j * 128)
                for j in range(2):
                    bx = 4 + j
                    pk = "pbuf%d" % j
                    W = H + NT
                    pj = pbuf[:, j, :]
                    P.op("pool", lambda e, j=j: e.tensor_copy(pbuf[:, j, 0:H], phist[:, l, j, :]), reads=["phist%d%d" % (l, j)], writes=[pk], n=15)
                    P.op("act", lambda e, j=j, bx=bx: e.copy(pbuf[:, j, H:H + NT], B[bx][:, 0:NT]), reads=[BK[bx]], writes=[pk], n=NT)
                    P.op("act", gas(sgz[:, j, 0:NT], B[bx][:, 256:256 + NT]), reads=[BK[bx]], writes=["sgz%d" % j], n=NT)
                    P.op("pool", lambda e, j=j: e.tensor_copy(phist[:, l, j, :], pbuf[:, j, NT:NT + H]), reads=[pk], writes=["phist%d%d" % (l, j)], n=15)
                    sAj, sBj = sA[:, 0, :], sB[:, 0, :]
                    ak, bk_ = "sA", "sB"
                    P.op("pool", lambda e, pj=pj, W=W, sAj=sAj: e.tensor_add(sAj[:, 1:W], pj[:, 1:W], pj[:, 0:W - 1]), reads=[pk], writes=[ak], n=W)
                    if j == 0:
                        P.op("pool", lambda e, W=W, sAj=sAj, sBj=sBj: e.tensor_add(sBj[64:128, 3:W], sAj[64:128, 3:W], sAj[64:128, 1:W - 2]),
                             reads=[ak], writes=[bk_], n=W)
                    else:
                        P.op("pool", lambda e, W=W, sAj=sAj, sBj=sBj: e.tensor_add(sBj[:, 3:W], sAj[:, 3:W], sAj[:, 1:W - 2]), reads=[ak], writes=[bk_], n=W)
                        P.op("pool", lambda e, W=W, sAj=sAj, sBj=sBj: e.tensor_add(sAj[:, 7:W], sBj[:, 7:W], sBj[:, 3:W - 4]), reads=[bk_], writes=[ak], n=W)
                        P.op("pool", lambda e, W=W, sAj=sAj, sBj=sBj: e.tensor_add(sBj[64:128, 15:W], sAj[64:128, 15:W], sAj[64:128, 7:W - 8]),
                             reads=[ak], writes=[bk_], n=W)
                    for (h0, src, skey) in ((0, sAj, ak), (64, sBj, bk_)):
                        P.op("dve", lambda e, h0=h0, src=src, j=j, pj=pj: e.scalar_tensor_tensor(
                            out=dT[h0:h0 + 64, j, 0:NT], in0=src[h0:h0 + 64, H:H + NT], scalar=invw[h0:h0 + 64, j:j + 1],
                            in1=pj[h0:h0 + 64, H:H + NT], op0=ALU.mult, op1=ALU.subtract),
                            reads=[skey, pk, "invw"], writes=["dT%d_%d" % (j, h0)], n=NT)
                        if first:
                            n1 = min(H, NT)
                            P.op("dve", lambda e, h0=h0, src=src, j=j, n1=n1: e.tensor_mul(
                                src[h0:h0 + 64, H:H + n1], src[h0:h0 + 64, H:H + n1], invcnt[h0:h0 + 64, j, 0:n1]),
                                reads=[skey, "invcnt"], writes=[skey], n=15)
                            P.op("dve", lambda e, h0=h0, src=src, j=j, n1=n1, pj=pj: e.tensor_sub(
                                dT[h0:h0 + 64, j, 0:n1], src[h0:h0 + 64, H:H + n1], pj[h0:h0 + 64, H:H + n1]),
                                reads=[skey, pk, "dT%d_%d" % (j, h0)], writes=["dT%d_%d" % (j, h0)], n=15)
                    if emit_state is not None:
                        pe_group([lambda e, j=j: e.transpose(B[4][0:H, 0:128], pbuf[:, j, NT:NT + H], identf[:])],
                                 [pk, "identf"], [BK[4]], 512)
                        P.op("dve", lambda e, j=j: e.tensor_copy(stgp[0:H, j * 128:(j + 1) * 128], B[4][0:H, 0:128]),
                             reads=[BK[4]], writes=["hxs0"], n=128)
                if emit_state is not None:
                    P.dma("sp", "stop", emit_state[0], stgp[0:H, 0:256], reads=["hxs0"])

            def conv_chunk(j, bx, by):
                fm_mm(bx, 0, C_GC + j * 128)
                fm_mm(bx, 256, C_HX + j * 128)
                fm_mm(by, 0, C_GB + j * 128)
                fm_mm(by, 256, C_ZC + j * 128)
                cp = 0
                C = "%d" % cp
                hx_, q_, ac_, sg_ = hxs[:, cp, :], qbuf[:, cp, :], acc[:, cp, :], sgc[:, cp, :]
                P.op("act", lambda e: e.copy(hx_[:, 0:NT], B[bx][:, 256:256 + NT]), reads=[BK[bx]], writes=["hxs" + C], n=NT)
                P.op("act", gas(sg_[:, 0:NT], B[by][:, 256:256 + NT]), reads=[BK[by]], writes=["sgc" + C], n=NT)
                P.op("dve", lambda e: e.tensor_mul(sg_[:, 0:NT], B[by][:, 0:NT], sg_[:, 0:NT]), reads=[BK[by], "sgc" + C], writes=["sgc" + C], n=NT)
                P.op("pool", lambda e: e.tensor_copy(q_[:, 0:2], qhist[:, l, j, :]), reads=["qhist%d%d" % (l, j)], writes=["qbuf" + C], n=2)
                P.op("dve", lambda e: e.tensor_mul(q_[:, 2:2 + NT], B[bx][:, 0:NT], hx_[:, 0:NT]),
                     reads=[BK[bx], "hxs" + C, "qbuf" + C], writes=["qbuf" + C], n=NT)
                P.op("pool", lambda e: e.tensor_copy(qhist[:, l, j, :], q_[:, NT:NT + 2]), reads=["qbuf" + C], writes=["qhist%d%d" % (l, j)], n=2)
                P.op("act", lambda e: e.activation(out=ac_[:, 0:NT], in_=q_[:, 0:NT], func=AF.Copy, scale=wc[:, l, j, 0:1]),
                     reads=["qbuf" + C, "wc"], writes=["acc" + C], n=NT)
                for tp in (1, 2):
                    P.op("dve", lambda e, tp=tp: e.scalar_tensor_tensor(
                        out=ac_[:, 0:NT], in0=q_[:, tp:tp + NT], scalar=wc[:, l, j, tp:tp + 1], in1=ac_[:, 0:NT],
                        op0=ALU.mult, op1=ALU.add), reads=["qbuf" + C, "wc", "acc" + C], writes=["acc" + C], n=NT)
                P.op("pool", lambda e: e.tensor_mul(ycTp[:, 2 + j, 0:NT], ac_[:, 0:NT], sg_[:, 0:NT]),
                     reads=["acc" + C, "sgc" + C], writes=[YK + "_%d" % (2 + j)], n=NT)
                if emit_state is not None:
                    pe_group([lambda e: e.transpose(B[4][0:2, 0:128], q_[:, NT:NT + 2], identf[:])],
                             ["qbuf" + C, "identf"], [BK[4]], 512)
                    P.op("dve", lambda e: e.tensor_copy(stgc[0:2, j * 128:(j + 1) * 128], B[4][0:2, 0:128]),
                         reads=[BK[4]], writes=["pbuf0", "pbuf1"], n=128)

            def mix(s):
                r0, nr = subs[s]
                bm = 2 + s
                S = "%d" % s
                pe_group([lambda e, h=h, nr=nr, bm=bm: e.matmul(B[bm][:nr, h * 64:(h + 1) * 64], WT[l][:nr, h, :nr],
                                                                vnb[s][:nr, h * 64:(h + 1) * 64], start=True, stop=True)
                          for h in range(6)], ["WT%d" % l, "vnb" + S], [BK[bm]], 6 * 64)
                P.op("dve", lambda e, nr=nr, bm=bm: e.tensor_add(nv[s][:nr, :], B[bm][:nr, 0:384], bsb[:nr, l, :]),
                     reads=[BK[bm], "bsb"], writes=["nv" + S], n=384)
                P.op("pool", lambda e, nr=nr: e.tensor_mul(yg[s][:nr, :], nv[s][:nr, :], sgug[s][:nr, :]),
                     reads=["nv" + S, "sgug" + S], writes=["yg" + S], n=384)

            def ytr(s, bank):
                r0, nr = subs[s]
                S = "%d" % s
                pe_group([lambda e, c=c, nr=nr: e.transpose(Bb[bank][:, c * 128:c * 128 + nr], yg[s][:nr, c * 128:(c + 1) * 128],
                                                          ident[:nr, :nr]) for c in range(3)],
                         ["yg" + S, "ident"], [BK[bank]], 384)
                P.op("act", lambda e, r0=r0, nr=nr: e.copy(ycTp[:, 5:8, r0:r0 + nr],
                                                        Bb[bank][:, 0:384].rearrange("p (c t) -> p c t", c=3)[:, :, 0:nr]),
                     reads=[BK[bank]], writes=[YK + "_g%d" % s], n=3 * nr)

            def pool_mm():
                pe_group([lambda e, j=j: e.matmul(B[4][:, j * 256:j * 256 + NT], wpbd[l][:, j, :], dT[:, j, 0:NT], start=True, stop=True)
                          for j in range(2)], ["wpbd%d" % l, "dT0_0", "dT0_64", "dT1_0", "dT1_64"], [BK[4]], 2 * NT)
                for j in range(2):
                    P.op("dve", lambda e, j=j: e.scalar_tensor_tensor(
                        out=ycTp[:, j, 0:NT], in0=B[4][:, j * 256:j * 256 + NT], scalar=psc[:, l, j:j + 1], in1=sgz[:, j, 0:NT],
                        op0=ALU.mult, op1=ALU.mult), reads=[BK[4], "sgz%d" % j, "psc"], writes=[YK + "_%d" % j], n=NT)

            tm(0)
            if len(subs) > 1:
                tm(1)
            pool_chunks()
            conv_chunk(0, 6, 7)
            mix(0)
            conv_chunk(1, 0, 1)
            pool_mm()
            if len(subs) > 1:
                mix(1)
            conv_chunk(2, 6, 7)
            if emit_state is not None:
                P.dma("sp", "stoc", emit_state[1], stgc[0:2, 0:384], reads=["pbuf0", "pbuf1"])
            ytr(0, 5)
            if len(subs) > 1:
                ytr(1, 4)

            YKA = [YK + "_%d" % c for c in range(5)] + [YK + "_g%d" % s for s in range(len(subs))]
            obanks = [(0, 1), (4, 5)]
            for s, (r0, nr) in enumerate(subs):
                xk = XK[s]
                for hf in range(2):
                    bo = obanks[s][hf]
                    korder = [0, 1, 2, 3, 5, 6, 7, 4]
                    pe_group([lambda e, k=k, i=i, bo=bo, hf=hf, r0=r0, nr=nr: e.matmul(
                        B[bo][:nr, :], ycTp[:, k, r0:r0 + nr], wout[l][:, k, hf * 512:(hf + 1) * 512], start=(i == 0), stop=(i == 7))
                        for i, k in enumerate(korder)], ["wout%d_0" % l, "wout%d_1" % l] + YKA, [BK[bo]], 8 * 512)
                    P.op("dve", lambda e, bo=bo, hf=hf, nr=nr: e.tensor_mul(B[bo][:nr, :], B[bo][:nr, :],
                                                                         gate_bc[:nr, gslot, hf * 512:(hf + 1) * 512]),
                         reads=[BK[bo], "gate_bc%d" % gslot], writes=[BK[bo]], n=512)
                    P.op("dve", lambda e, s=s, hf=hf, nr=nr, bo=bo: e.tensor_add(xb[:nr, s, hf * 512:(hf + 1) * 512],
                                                                               B[bo][:nr, :],
                                                                               xb[:nr, s, hf * 512:(hf + 1) * 512]),
                         reads=[xk, BK[bo]], writes=[xk], n=512)
                if last_layer:
                    for hf in range(2):
                        jb_ = obanks[s][hf]
                        P.op("act", lambda e, s=s, nr=nr, jb_=jb_, hf=hf: e.activation(
                            out=B[jb_][:nr, :], in_=xb[:nr, s, hf * 512:(hf + 1) * 512], func=AF.Square,
                            scale=1.0 / 32.0, accum_out=msf[:nr, 2 * s + hf:2 * s + hf + 1]),
                            reads=[xk], writes=[BK[jb_], "msf%d_%d" % (s, hf)], n=512)
                    P.op("pool", lambda e, s=s, nr=nr: e.tensor_add(rsf[:nr, s:s + 1], msf[:nr, 2 * s:2 * s + 1], msf[:nr, 2 * s + 1:2 * s + 2]),
                         reads=["msf%d_0" % s, "msf%d_1" % s], writes=["rsf%d" % s], n=1)
                    rstd_op(rsf[:nr, s:s + 1], rsf[:nr, s:s + 1], ["rsf%d" % s], ["rsf%d" % s])
                    P.op("dve", lambda e, s=s, nr=nr: e.scalar_tensor_tensor(out=xb[:nr, s, :], in0=xb[:nr, s, :],
                                                                          scalar=rsf[:nr, s:s + 1], in1=gfin[:nr, :],
                                                                          op0=ALU.mult, op1=ALU.mult),
                         reads=[xk, "rsf%d" % s, "gfin"], writes=[xk], n=1024)
                    P.dma("sp", "yout", out_ap[t0 + r0:t0 + r0 + nr, :], xb[:nr, s, :], reads=[xk], nbytes=nr * 4096)

        def gate_load(seq, l, slot, nparts=128):
            P.dma("sp", "gate%d" % slot, gate_bc[0:nparts, slot, :], gscr[l, seq:seq + 1, :].broadcast_to([nparts, D]),
                  reads=["gscr%d%d" % (l, jb) for jb in range(8, 12)], writes=["gate_bc%d" % slot], nbytes=4096 * nparts)

        def seq_begin(seq):
            for l in range(DEPTH):
                gate_load(seq, l, l)
            PH = ["phist%d%d" % (l, j) for l in range(DEPTH) for j in range(2)]
            QH = ["qhist%d%d" % (l, j) for l in range(DEPTH) for j in range(3)]
            P.op("pool", lambda e: e.memset(phist[:], 0.0), reads=PH, writes=PH, n=60)
            P.op("pool", lambda e: e.memset(qhist[:], 0.0), reads=QH, writes=QH, n=12)

        def sample_hist_load(seq, l):
            b = seq - 2
            P.dma("sp", "stip", stgp[0:15, 0:256], stp[l, b], writes=["hxs0"])
            for j in range(2):
                pe_group([lambda e, j=j: e.transpose(B[4][:, 0:15], stgp[0:15, j * 128:(j + 1) * 128], identf[0:15, 0:15])],
                         ["hxs0", "identf"], [BK[4]], 64)
                P.op("dve", lambda e, l=l, j=j: e.tensor_copy(phist[:, l, j, :], B[4][:, 0:15]),
                     reads=[BK[4]], writes=["phist%d%d" % (l, j)], n=15)
            P.dma("sp", "stic", stgc[0:2, 0:384], stc[l, b], writes=["pbuf0", "pbuf1"])
            for j in range(3):
                pe_group([lambda e, j=j: e.transpose(B[4][:, 0:2], stgc[0:2, j * 128:(j + 1) * 128], identf[0:2, 0:2])],
                         ["pbuf0", "pbuf1", "identf"], [BK[4]], 64)
                P.op("dve", lambda e, l=l, j=j: e.tensor_copy(qhist[:, l, j, :], B[4][:, 0:2]),
                     reads=[BK[4]], writes=["qhist%d%d" % (l, j)], n=2)

        import os as _os
        KSTOP = int(_os.environ.get("KSTOP", "100000"))
        _tl = tile_layer
        _cnt = [0]

        def tile_layer(*a):
            _cnt[0] += 1
            if _cnt[0] <= KSTOP:
                _tl(*a)
        ntile = SEQ // TT
        gt = 0
        for seq in range(2):
            seq_begin(seq)
            for t2 in range(0, ntile, 2):
                tl = [t for t in (t2, t2 + 1) if t < ntile]
                pps = {}
                for t in tl:
                    pp = gt % NXB
                    gt += 1
                    pps[t] = pp
                    P.dma("sp", "xin%d" % pp, xbuf[pp][:], xp[seq, t * TT:(t + 1) * TT, :].rearrange("(s p) d -> p s d", p=128),
                          writes=["x%d_0" % pp, "x%d_1" % pp], nbytes=1048576)
                for l in range(DEPTH):
                    for t in tl:
                        es = None
                        if t == ntile - 1:
                            es = (npp[l, seq], ncp[l, seq])
                        tile_layer(seq, l, TT, [(0, 128), (128, 128)], t == 0, es, False, l == DEPTH - 1, y_p[seq], t * TT, pps[t])
        spp = {}
        for seq in (2, 3):
            pp = gt % NXB
            gt += 1
            spp[seq] = pp
            P.dma("sp", "xin%d" % pp, xbuf[pp][0:NS, 0, :], xs[seq - 2], writes=["x%d_0" % pp, "x%d_1" % pp])
        for l in range(DEPTH):
            for seq in (2, 3):
                sample_hist_load(seq, l)
                gate_load(seq, l, seq - 2, nparts=NS)
                es = (nps[l, seq - 2], ncs[l, seq - 2])
                tile_layer(seq, l, NS, [(0, NS)], False, es, True, l == DEPTH - 1, y_s[seq - 2], 0, spp[seq], seq - 2)

        P.emit(reorder=REORDER)
    return nc


def make_in_maps(inputs, n_cores=8):
    g = lambda k: np.ascontiguousarray(np.asarray(inputs[k], dtype=np.float32))
    xp, xs, stp, stc = g("x_prompt"), g("x_sample"), g("state_pool"), g("state_conv")
    cp, cs = g("c_prompt"), g("c_sample")
    shared = {k: g(k) for k in ("norm_g", "w_ada", "b_ada", "w_in", "w_pool", "pool_scale", "w_conv",
                                "v_norm_g", "v_norm_b", "w_s", "b_s", "w_out")}
    shared["final_norm_g"] = g("final_norm_g").reshape(1, -1)
    maps = []
    for c in range(n_cores):
        sl = slice(2 * c, 2 * c + 2)
        m = dict(shared)
        m["xp"] = np.ascontiguousarray(xp[sl])
        m["xs"] = np.ascontiguousarray(xs[sl])
        m["stp"] = np.ascontiguousarray(stp[:, sl])
        m["stc"] = np.ascontiguousarray(stc[:, sl])
        m["c4"] = np.ascontiguousarray(np.concatenate([cp[sl], cs[sl]], axis=0))
        maps.append(m)
    return maps


def gather(results):
    cat = lambda k, ax: np.concatenate([np.asarray(r[k], dtype=np.float32) for r in results], axis=ax)
    return (cat("y_p", 0), cat("y_s", 0), cat("npp", 1), cat("ncp", 1), cat("nps", 1), cat("ncs", 1), cat("nvs", 1))


def kernel(**inputs):
    n = 8
    seq = inputs["x_prompt"].shape[1]
    ns = inputs["x_sample"].shape[1]
    nc = build_nc(seq, ns)
    res = run_bass_kernel_spmd(nc, make_in_maps(inputs, n), core_ids=list(range(n)))
    return gather(res.results)
```

```python
import contextlib
import numpy as np
import concourse.bass as bass
import concourse.mybir as mybir
from concourse.bass_utils import run_bass_kernel_spmd

F32 = mybir.dt.float32
BF16 = mybir.dt.bfloat16
ALU = mybir.AluOpType
AF = mybir.ActivationFunctionType

D = 1024
DEPTH = 2
PW = 3200
EPS = 1e-6
GS = 1.702
TT = 256
NXB = 2
NHT = 1
EXPERIMENT_SHRINK = False
REORDER = True
PRIO_CP = True
PRIO_BUCKET = 40
HOP_NS = 400.0
C_P, C_ZP, C_GB, C_GC, C_HX, C_ZC, C_U, C_V, C_ZG = 0, 256, 512, 896, 1280, 1664, 2048, 2432, 2816


class Op:
    __slots__ = ("idx", "eng", "fn", "deps", "order", "inc", "dur", "lat", "dsem", "seq", "dcount", "done")

    def __init__(self, idx, eng, fn, deps, inc, dur, lat=0.0, dsem=None):
        self.idx, self.eng, self.fn, self.deps, self.inc = idx, eng, fn, deps, inc
        self.order = []
        self.dur, self.lat, self.dsem = dur, lat, dsem
        self.seq = 0
        self.dcount = 0
        self.done = 0.0


class Prog:
    ENGS = ("pe", "act", "dve", "pool", "sp")

    def __init__(self, nc, stack):
        self.nc = nc
        self.stack = stack
        self.all = []
        self.sem = {e: stack.enter_context(nc.semaphore("c_" + e)) for e in self.ENGS}
        self.buf = {}
        self.dsem = {}
        self.dlast = {}
        self.groups = {}

    def _deps(self, eng, reads, writes):
        deps = []
        for k in reads:
            st = self.buf.get(k)
            if st and st[0] is not None:
                deps.append(st[0])
        for k in writes:
            st = self.buf.get(k)
            if st:
                if st[0] is not None:
                    deps.append(st[0])
                deps.extend(st[1].values())
        return deps

    def _mark(self, me, reads, writes):
        for k in reads:
            st = self.buf.setdefault(k, [None, {}])
            st[1][id(me)] = me
        for k in writes:
            self.buf[k] = [me, {}]

    def op(self, eng, fn, reads=(), writes=(), n=256, dur=None):
        if dur is None:
            dur = {"act": (130 + 0.55 * n) if n <= 512 else (200 + 1.0 * n), "dve": 200 + 0.85 * n,
                   "pool": 180 + 1.8 * n, "pe": 60 + n / 2.0}[eng]
        deps = self._deps(eng, reads, writes)
        o = Op(len(self.all), eng, fn, deps, True, dur)
        self.all.append(o)
        self._mark(o, reads, writes)
        return o

    def dma(self, q, name, out, in_, reads=(), writes=(), batch=None, nbytes=65536, **kw):
        if name not in self.dsem:
            self.dsem[name] = self.stack.enter_context(self.nc.semaphore("d_" + name))
        deps = self._deps(q, reads, writes)

        def fn(e, out=out, in_=in_, kw=kw):
            return e.dma_start(out=out, in_=in_, **kw)
        o = Op(len(self.all), q, fn, deps, False, 150.0 if q == "sp" else 1200.0, 2000.0 + nbytes / 250.0, name)
        if batch is not None:
            g = self.groups.setdefault((name, batch), [])
            o.deps = [d for d in deps if d not in g]
            g.append(o)
        prev = self.dlast.get(name)
        if prev is not None:
            o.order.append(prev)
        self.dlast[name] = o
        self.all.append(o)
        self._mark(o, reads, writes)
        return o

    def schedule(self):
        import heapq
        ops = self.all
        member = {}
        for g in self.groups.values():
            for o in g:
                member[id(o)] = g
        for o in ops:
            extra = []
            for d in o.deps:
                g = member.get(id(d))
                if g is not None and o not in g:
                    extra.extend(g)
            if extra:
                o.deps = list(o.deps) + extra
        nsucc = [[] for _ in ops]
        indeg = [0] * len(ops)
        for o in ops:
            ds = set(id(d) for d in o.deps) | set(id(d) for d in o.order)
            preds = {d.idx: d for d in list(o.deps) + list(o.order)}
            indeg[o.idx] = len(preds)
            for d in preds.values():
                nsucc[d.idx].append(o)
        heaps = {e: [] for e in self.ENGS}
        prio = [0.0] * len(ops)
        if PRIO_CP:
            for o in reversed(ops):
                best = 0.0
                for s_ in nsucc[o.idx]:
                    if prio[s_.idx] > best:
                        best = prio[s_.idx]
                prio[o.idx] = best + o.dur + (o.lat if o.dsem else 120.0)
        ready_t = [0.0] * len(ops)
        for o in ops:
            if indeg[o.idx] == 0:
                heapq.heappush(heaps[o.eng], (0.0, o.idx))
        t_eng = {e: 0.0 for e in self.ENGS}
        self.gorder = []
        dma_free = 0.0
        order = {e: [] for e in self.ENGS}
        nleft = len(ops)
        while nleft:
            best = None
            for e in self.ENGS:
                h = heaps[e]
                if not h:
                    continue
                rt, idx = h[0]
                if rt <= t_eng[e]:
                    cand = []
                    while h and h[0][0] <= t_eng[e]:
                        cand.append(heapq.heappop(h))
                    if PRIO_CP:
                        pick = min(cand, key=lambda c: (c[1] // PRIO_BUCKET, -prio[c[1]], c[1]))
                    else:
                        pick = min(cand, key=lambda c: c[1])
                    for c in cand:
                        if c is not pick:
                            heapq.heappush(h, c)
                    start = t_eng[e]
                else:
                    pick = heapq.heappop(h)
                    start = rt
                if best is None or start < best[0]:
                    if best is not None:
                        heapq.heappush(heaps[best[1]], best[2])
                    best = (start, e, pick)
                else:
                    heapq.heappush(h, pick)
            start, e, pick = best
            o = ops[pick[1]]
            self.gorder.append(o)
            t_eng[e] = start + o.dur
            if o.dsem:
                xfer = (o.lat - 2000.0) * 250.0 / 270.0
                dma_free = max(dma_free, start) + xfer
                o.done = max(start + 2000.0, dma_free + 1000.0)
            else:
                o.done = start + o.dur + HOP_NS
            order[e].append(o)
            nleft -= 1
            for s in nsucc[o.idx]:
                ready_t[s.idx] = max(ready_t[s.idx], o.done)
                indeg[s.idx] -= 1
                if indeg[s.idx] == 0:
                    heapq.heappush(heaps[s.eng], (ready_t[s.idx], s.idx))
        self.est_ns = max(t_eng.values())
        return order

    def emit(self, final_sems=("yout", "stop", "stoc", "nvo"), reorder=True):
        nc = self.nc
        if reorder:
            order = self.schedule()
        else:
            order = {e: [o for o in self.all if o.eng == e] for e in self.ENGS}
        dcnt = {}
        for e in self.ENGS:
            c = 0
            for o in order[e]:
                if o.dsem:
                    dcnt[o.dsem] = dcnt.get(o.dsem, 0) + 16
                    o.dcount = dcnt[o.dsem]
                else:
                    c += 1
                    o.seq = c
        for g in self.groups.values():
            m = max(o.dcount for o in g)
            for o in g:
                o.dcount = m
        gorder = self.gorder if reorder else list(self.all)
        know = {e: {} for e in self.ENGS}
        clock = {}
        waits = {}
        for o in gorder:
            e = o.eng
            K = know[e]
            needs = []
            for d in o.deps:
                if d.dsem:
                    needs.append((("d", d.dsem), self.dsem[d.dsem], d.dcount, d))
                else:
                    if d.eng == e and e == "pe":
                        continue
                    needs.append((("e", d.eng), self.sem[d.eng], d.seq, d))
            w = []
            for key, sobj, v, d in sorted(needs, key=lambda t: -t[2]):
                if K.get(key, 0) < v:
                    w.append((sobj, v))
                    K[key] = v
                    for k2, v2 in clock[id(d)].items():
                        if K.get(k2, 0) < v2:
                            K[k2] = v2
            waits[id(o)] = w
            c = dict(K)
            if o.dsem:
                if c.get(("d", o.dsem), 0) < o.dcount:
                    c[("d", o.dsem)] = o.dcount
            else:
                c[("e", e)] = max(c.get(("e", e), 0), o.seq)
            clock[id(o)] = c
        self.n_waits = sum(len(w) for w in waits.values())
        handles = {"pe": "tensor", "act": "scalar", "dve": "vector", "pool": "gpsimd", "sp": "sync"}
        with nc.Block() as block:
            for ename in self.ENGS:
                ops = order[ename]
                if not ops:
                    continue

                def body(e, ops=ops, ename=ename):
                    own = self.sem[ename]
                    for o in ops:
                        for sobj, v in waits[id(o)]:
                            e.wait_ge(sobj, v)
                        ins = o.fn(e)
                        if o.dsem:
                            ins.then_inc(self.dsem[o.dsem], 16)
                        else:
                            ins.then_inc(own, 1)
                    if ename == "sp":
                        for n in final_sems:
                            if n in dcnt:
                                e.wait_ge(self.dsem[n], dcnt[n])
                getattr(block, handles[ename])(body)


def build_nc(SEQ=4096, NS=16):
    nc = bass.Bass("TRN2", target_bir_lowering=False)
    dt_in = lambda n, s: nc.dram_tensor(n, s, F32, kind="ExternalInput").ap()
    dt_out = lambda n, s: nc.dram_tensor(n, s, F32, kind="ExternalOutput").ap()
    xp = dt_in("xp", [2, SEQ, D])
    xs = dt_in("xs", [2, NS, D])
    stp = dt_in("stp", [DEPTH, 2, 15, 256])
    stc = dt_in("stc", [DEPTH, 2, 2, 384])
    c4 = dt_in("c4", [4, D])
    norm_g = dt_in("norm_g", [DEPTH, D])
    w_ada = dt_in("w_ada", [DEPTH, D, 3 * D])
    b_ada = dt_in("b_ada", [DEPTH, 3 * D])
    w_in = dt_in("w_in", [DEPTH, D, PW])
    w_pool = dt_in("w_pool", [DEPTH, 4, 64, 64])
    pool_scale = dt_in("pool_scale", [DEPTH, 256])
    w_conv = dt_in("w_conv", [DEPTH, 3, 384])
    v_norm_g = dt_in("v_norm_g", [DEPTH, 384])
    v_norm_b = dt_in("v_norm_b", [DEPTH, 384])
    w_s = dt_in("w_s", [DEPTH, 6, 128, 128])
    b_s = dt_in("b_s", [DEPTH, 6, 128])
    w_out = dt_in("w_out", [DEPTH, D, D])
    fng = dt_in("final_norm_g", [1, D])
    y_p = dt_out("y_p", [2, SEQ, D])
    y_s = dt_out("y_s", [2, NS, D])
    npp = dt_out("npp", [DEPTH, 2, 15, 256])
    ncp = dt_out("ncp", [DEPTH, 2, 2, 384])
    nps = dt_out("nps", [DEPTH, 2, 15, 256])
    ncs = dt_out("ncs", [DEPTH, 2, 2, 384])
    nvs = dt_out("nvs", [DEPTH, 2, NS, 384])
    gscr = nc.dram_tensor("gscr", [DEPTH, 4, D], F32, kind="Internal").ap()

    with contextlib.ExitStack() as st:
        P = Prog(nc, st)
        sb = lambda n, s, d=F32: st.enter_context(nc.sbuf_tensor(n, s, d))
        nc_ncd = st.enter_context(nc.allow_non_contiguous_dma(reason="tiny constant layouts"))
        B = [st.enter_context(nc.psum_tensor("B%d" % i, [128, 512], F32)) for i in range(8)]
        Bb = [b[:].bitcast(BF16) for b in B]
        BK = ["B%d" % i for i in range(8)]

        win = [sb("win%d" % l, [128, 8, PW], BF16) for l in range(DEPTH)]
        wout = [sb("wout%d" % l, [128, 8, D], BF16) for l in range(DEPTH)]
        xbuf = [sb("xbuf%d" % i, [128, 2, D]) for i in range(NXB)]
        xn = sb("xn", [128, 2, D], BF16)
        otmp = xn[:].rearrange("p s d -> p (s d)").bitcast(F32)
        hT = [sb("hT%d" % i, [128, 8, TT], BF16) for i in range(NHT)]
        ycT = [sb("ycT0", [128, 8, TT], BF16)]
        gate_bc = sb("gate_bc", [128, DEPTH, D])
        gfin = sb("gfin", [128, D]) if not EXPERIMENT_SHRINK else gate_bc[:, 0, :]
        pbuf = sb("pbuf", [128, 2, 15 + TT])
        sA = sb("sA", [128, 1, 15 + TT])
        sB = sb("sB", [128, 1, 15 + TT])
        dT = sb("dT", [128, 2, TT], BF16)
        sgz = sb("sgz", [128, 2, TT])
        phist = sb("phist", [128, DEPTH, 2, 15])
        hxs = sb("hxs", [128, 1, TT])
        qbuf = sb("qbuf", [128, 1, 2 + TT])
        acc = sb("acc", [128, 1, TT])
        sgc = sb("sgc", [128, 1, TT])
        qhist = sb("qhist", [128, DEPTH, 3, 2])
        sgug = [sb("sgug%d" % i, [128, 384]) for i in range(2)]
        nv = [sb("nv%d" % i, [128, 384]) for i in range(2)]
        vnb = [sb("vnb%d" % i, [128, 384], BF16) for i in range(2)]
        yg = [sb("yg%d" % i, [128, 384], BF16) for i in range(2)]
        bst = sb("bst", [128, 2, 6])
        mv = sb("mv", [128, 2, 2])
        rsv = sb("rsv", [128, 2])
        ms = sb("ms", [128, 2])
        msf = sb("msf", [128, 4])
        rsf = sb("rsf", [128, 2])
        rs = sb("rs", [128, 2])
        ident = sb("ident", [128, 128], BF16)
        identf = sb("identf", [128, 128])
        nhalf = sb("nhalf", [128, 1])
        barr = sb("barr", [128, 1])
        WT = [sb("WT%d" % l, [128, 6, 128], BF16) for l in range(DEPTH)]
        wstg = xbuf[0][:, 1, 0:768].rearrange("p (h s) -> p h s", h=6)
        wpbd = [sb("wpbd%d" % l, [128, 2, 128], BF16) for l in range(DEPTH)]
        gam = sb("gam", [128, DEPTH, 384])
        bet = sb("bet", [128, DEPTH, 384])
        bsb = sb("bsb", [128, DEPTH, 384])
        bs6 = sb("bs6", [128, DEPTH, 6])
        psc = sb("psc", [128, DEPTH, 2])
        wc = sb("wc", [128, DEPTH, 3, 3])
        ngT = sb("ngT", [128, DEPTH, 8])
        invw = sb("invw", [128, 2])
        invcnt = sb("invcnt", [128, 2, 15])
        gsT = sb("gsT", [128, DEPTH, 8, 4])
        shT = sb("shT", [128, DEPTH, 8, 4])
        c4t = xbuf[0][0:4, 0, :]
        scT = sb("scT", [128, 8, 4], BF16)
        mrow = otmp[0:4, 0:512].rearrange("p (a b) -> p a b", a=2)
        bblk = otmp[0:4, 512:1024].rearrange("p (a b) -> p a b", a=2)
        stgp = hxs[0:16, 0, :]
        stgc = pbuf[0:2, :, :].rearrange("p a b -> p (a b)")[:, 0:384]

        def pe_group(fns, reads, writes, n):
            def fn(e, fns=fns):
                ins = None
                for f in fns:
                    ins = f(e)
                return ins
            return P.op("pe", fn, reads=reads, writes=writes, dur=8.0 * len(fns) + n / 2.3)

        P.op("pool", lambda e: e.memset(identf[:], 0.0), writes=["identf"])
        P.op("pool", lambda e: e.affine_select(out=identf[:], in_=identf[:], pattern=[[-1, 128]],
                                               compare_op=ALU.not_equal, fill=1.0, base=0,
                                               channel_multiplier=1),
             reads=["identf"], writes=["identf"])
        P.op("dve", lambda e: e.tensor_copy(ident[:], identf[:]), reads=["identf"], writes=["ident"])
        P.op("pool", lambda e: e.memset(nhalf[:], -0.5), writes=["nhalf"])
        wins = (2, 4, 8, 16)
        for g, w in enumerate(wins):
            j, h0 = g // 2, (g % 2) * 64
            P.op("pool", lambda e, j=j, h0=h0, w=w: e.memset(invw[h0:h0 + 64, j:j + 1], 1.0 / w), writes=["invw"], n=1)
            P.op("pool", lambda e, j=j, h0=h0, w=w: e.memset(invcnt[h0:h0 + 64, j, :], 1.0 / w), writes=["invcnt"], n=15)
            for pos in range(w - 1):
                P.op("pool", lambda e, j=j, h0=h0, pos=pos: e.memset(invcnt[h0:h0 + 64, j, pos:pos + 1], 1.0 / (pos + 1)),
                     writes=["invcnt"], n=1)
        P.dma("sp", "cst", c4t[:], c4, writes=["x0_0"], batch=0)
        P.dma("sp", "cst", gfin[:], fng.broadcast_to([128, D]), writes=["gfin"], batch=0)
        for l in range(DEPTH):
            P.dma("sp", "cst", gam[:, l, :], v_norm_g[l:l + 1, :].broadcast_to([128, 384]), writes=["gam"], batch=0)
            P.dma("sp", "cst", bet[:, l, :], v_norm_b[l:l + 1, :].broadcast_to([128, 384]), writes=["bet"], batch=0)
            P.dma("sp", "cst", bs6[:, l, :], b_s[l].rearrange("h t -> t h"), writes=["bs6"], batch=0)
            P.dma("sp", "cst", psc[:, l, :], pool_scale[l].rearrange("(j p) -> p j", p=128), writes=["psc"], batch=0)
            for j in range(3):
                P.dma("sp", "cst", wc[:, l, j, :], w_conv[l][:, j * 128:(j + 1) * 128].rearrange("k p -> p k"), writes=["wc"], batch=0)
            P.dma("sp", "cst", ngT[:, l, :], norm_g[l].rearrange("(k p) -> p k", p=128), writes=["ngT"], batch=0)
        P.op("dve", lambda e: e.tensor_scalar_mul(psc[:], psc[:], GS), reads=["psc"], writes=["psc"], n=4)
        P.op("dve", lambda e: e.tensor_scalar_mul(wc[:], wc[:], GS), reads=["wc"], writes=["wc"], n=18)
        for l in range(DEPTH):
            P.op("dve", lambda e, l=l: e.tensor_copy(bsb[:, l, :].rearrange("p (h c) -> p h c", h=6),
                                                      bs6[:, l, :].unsqueeze(2).to_broadcast([128, 6, 64])),
                 reads=["bs6"], writes=["bsb"], n=384)
        P.op("act", lambda e: e.activation(out=c4t[:], in_=c4t[:], func=AF.Gelu_apprx_sigmoid, scale=1.0 / GS),
             reads=["x0_0"], writes=["x0_0"], n=1024)
        pe_group([lambda e, k=k: e.transpose(B[0][:, k * 4:(k + 1) * 4], c4t[0:4, k * 128:(k + 1) * 128], identf[0:4, 0:4])
                  for k in range(8)], ["x0_0", "identf"], [BK[0]], 128)
        P.op("dve", lambda e: e.tensor_copy(scT[:].rearrange("p k s -> p (k s)"), B[0][:, 0:32]), reads=[BK[0]], writes=["scT"], n=32)
        blk = 0
        for l in range(DEPTH):
            for jb in range(12):
                par = blk % 2
                HT0K = ["hT0_%d_%d" % (s_, k_) for s_ in range(2) for k_ in range(8)]
                YC0K = ["ycT0_%d" % c_ for c_ in range(5)] + ["ycT0_g0", "ycT0_g1"]
                x1v = xbuf[1][:].rearrange("p s d -> p (s d)").bitcast(BF16)
                stages = [(hT[0], HT0K), (ycT[0], YC0K),
                          (x1v[:, 0:2048].rearrange("p (k n) -> p k n", k=8), ["x1_0"]),
                          (x1v[:, 2048:4096].rearrange("p (k n) -> p k n", k=8), ["x1_1"])]
                sbuf_stage, skeys = stages[blk % 4]
                spar = blk % 4
                bx, by = 2 + par, 4 + par
                c0 = jb * 256
                last_wada = P.dma("pool", "wada%d" % spar, sbuf_stage[:], w_ada[l][:, c0:c0 + 256].rearrange("(k p) n -> p k n", p=128),
                                  writes=skeys, nbytes=1048576)
                P.dma("sp", "bblk%d" % par, bblk[:, par, :], b_ada[l:l + 1, c0:c0 + 256].broadcast_to([4, 256]),
                      writes=["bblk%d" % par])
                pe_group([lambda e, k=k, bx=bx, s_=sbuf_stage: e.matmul(B[bx][0:4, 0:256], scT[:, k, :], s_[:, k, :],
                                                                       start=(k == 0), stop=(k == 7)) for k in range(8)],
                         ["scT"] + skeys, [BK[bx]], 8 * 256)
                P.op("dve", lambda e, bx=bx, par=par: e.scalar_tensor_tensor(out=mrow[:, par, :], in0=B[bx][0:4, 0:256], scalar=GS,
                                                                          in1=bblk[:, par, :], op0=ALU.mult, op1=ALU.add),
                     reads=[BK[bx], "bblk%d" % par], writes=["mrow%d" % par])
                if jb < 8:
                    pe_group([lambda e, i=i, by=by, par=par: e.transpose(B[by][:, i * 4:(i + 1) * 4],
                                                                      mrow[0:4, par, i * 128:(i + 1) * 128], identf[0:4, 0:4])
                              for i in range(2)], ["mrow%d" % par, "identf"], [BK[by]], 64)
                    if jb < 4:
                        P.op("dve", lambda e, l=l, jb=jb, by=by: e.tensor_copy(
                            shT[:, l, 2 * jb:2 * jb + 2, :], B[by][:, 0:8].rearrange("p (i s) -> p i s", i=2)),
                            reads=[BK[by]], writes=["shT"], n=8)
                    else:
                        for i in range(2):
                            kk = 2 * (jb - 4) + i
                            P.op("dve", lambda e, l=l, kk=kk, i=i, by=by: e.scalar_tensor_tensor(
                                out=gsT[:, l, kk, :], in0=B[by][:, i * 4:(i + 1) * 4], scalar=1.0,
                                in1=ngT[:, l, kk:kk + 1].to_broadcast([128, 4]), op0=ALU.add, op1=ALU.mult),
                                reads=[BK[by], "ngT"], writes=["gsT"], n=4)
                else:
                    P.dma("sp", "gscr%d" % par, gscr[l, :, (jb - 8) * 256:(jb - 7) * 256], mrow[:, par, :],
                          reads=["mrow%d" % par], writes=["gscr%d%d" % (l, jb)])
                blk += 1

        for l in range(DEPTH):
            P.dma("sp", "x0_1", wstg[:], w_s[l].rearrange("h t s -> t h s"), reads=[], writes=["x0_1"], nbytes=393216)
            for h in range(6):
                bk = 2 + (h % 2)
                pe_group([lambda e, h=h, bk=bk: e.transpose(B[bk][:, 0:128], wstg[:, h, :], identf[:])],
                         ["x0_1", "identf"], [BK[bk]], 512)
                P.op("dve", lambda e, l=l, h=h, bk=bk: e.tensor_copy(WT[l][:, h, :], B[bk][:, 0:128]),
                     reads=[BK[bk]], writes=["WT%d" % l], n=128)
            P.op("pool", lambda e, l=l: e.memset(WT[l][64:128, :, 0:64], 0.0), reads=["WT%d" % l], writes=["WT%d" % l], n=384)
            P.op("pool", lambda e: e.memset(wstg[:, 0:2, :], 0.0), writes=["x0_1"], n=256)
            for g in range(4):
                j, h0 = g // 2, (g % 2) * 64
                P.dma("sp", "x0_1", wstg[h0:h0 + 64, j, h0:h0 + 64], w_pool[l, g], reads=["x0_1"], writes=["wstgd%d" % g])
            P.op("dve", lambda e, l=l: e.tensor_copy(wpbd[l][:], wstg[:, 0:2, :]),
                 reads=["x0_1", "wstgd0", "wstgd1", "wstgd2", "wstgd3"], writes=["wpbd%d" % l], n=256)

        P.op("pool", lambda e: e.memset(barr[:], 0.0), writes=["mrow0", "mrow1", "bblk0", "bblk1", "xn0", "xn1"], n=1)

        prev = last_wada
        for l in range(DEPTH):
            for k in range(8):
                o = P.dma("pool", "win%d" % l, win[l][:, k, :], w_in[l][k * 128:(k + 1) * 128, :], writes=["win%d_%d" % (l, k)],
                          batch=0, nbytes=1638400)
                o.order.append(prev)
                prev = o
            for k in range(0, 8, 4):
                o = P.dma("pool", "wout%d" % l, wout[l][:, k:k + 4, :],
                          w_out[l][k * 128:(k + 4) * 128, :].rearrange("(k p) n -> p k n", p=128), writes=["wout%d_%d" % (l, k // 4)],
                          batch=0, nbytes=2097152)
                o.order.append(prev)
                prev = o

        def rstd_op(src, dst, reads, writes):
            P.op("pool", lambda e: e.tensor_scalar_add(dst, src, EPS), reads=reads, writes=writes, n=1)
            P.op("pool", lambda e: e.tensor_tensor(dst, dst, nhalf[:dst.shape[0], :], ALU.pow),
                 reads=writes + ["nhalf"], writes=writes, dur=490.0)

        def gas(out, in_):
            return lambda e: e.activation(out=out, in_=in_, func=AF.Gelu_apprx_sigmoid, scale=1.0 / GS)

        import os as _os2
        KPH = int(_os2.environ.get("KPH", "100"))
        KQ = int(_os2.environ.get("KQ", "100"))

        def tile_layer(seq, l, NT, subs, first, emit_state, is_sample, last_layer, out_ap, t0, pp, gslot=None):
            if gslot is None:
                gslot = l
            H = 15
            xb, hTp, ycTp = xbuf[pp], hT[pp % NHT], ycT[0]
            XK = ["x%d_%d" % (pp, s) for s in range(2)]
            HK, YK = "hT%d" % (pp % NHT), "ycT0"
            WK = ["win%d_%d" % (l, k) for k in range(8)]

            def fm_mm(bank, off, c0):
                pe_group([lambda e, k=k: e.matmul(B[bank][:, off:off + NT], win[l][:, k, c0:c0 + 128], hTp[:, k, 0:NT],
                                                  start=(k == 0), stop=(k == 7)) for k in range(8)],
                         WK + [HK], [BK[bank]], 8 * NT)

            for s, (r0, nr) in enumerate(subs):
                xk = XK[s]
                P.op("act", lambda e, s=s, nr=nr: e.activation(out=xn[:nr, s, :], in_=xb[:nr, s, :], func=AF.Square,
                                                             scale=1.0 / 32.0, accum_out=ms[:nr, s:s + 1]),
                     reads=[xk], writes=["xn%d" % s, "ms%d" % s], n=1024)
                if KQ <= 0:
                    continue
                rstd_op(ms[:nr, s:s + 1], rs[:nr, s:s + 1], ["ms%d" % s], ["rs%d" % s])
                if KQ <= 1:
                    continue
                P.op("act", lambda e, s=s, nr=nr: e.activation(out=xn[:nr, s, :], in_=xb[:nr, s, :], func=AF.Copy,
                                                             scale=rs[:nr, s:s + 1]),
                     reads=[xk, "rs%d" % s], writes=["xn%d" % s], n=1024)
                if KQ <= 2:
                    continue
                tbank = (2 + s, 6 + s)
                for hb in range(2):
                    pe_group([lambda e, s=s, nr=nr, k=k, tb=tbank[hb]: e.transpose(Bb[tb][:, (k % 4) * 128:(k % 4) * 128 + nr],
                                                                            xn[:nr, s, k * 128:(k + 1) * 128], ident[:nr, :nr])
                              for k in range(4 * hb, 4 * hb + 4)], ["xn%d" % s, "ident"], [BK[tbank[hb]]], 4 * 128)
                if KQ <= 3:
                    continue
                for k in range(8):
                    hb = k // 4
                    o = hTp[:, k, r0:r0 + nr]
                    i_ = Bb[tbank[hb]][:, (k % 4) * 128:(k % 4) * 128 + nr]
                    g_ = gsT[:, l, k, seq:seq + 1]
                    h_ = shT[:, l, k, seq:seq + 1]
                    if hb == 0:
                        P.op("act", lambda e, o=o, i_=i_, g_=g_, h_=h_: e.activation(out=o, in_=i_, func=AF.Identity, bias=h_, scale=g_),
                             reads=[BK[tbank[hb]], "gsT", "shT"], writes=[HK + "_%d_%d" % (s, k)], n=nr)
                    else:
                        P.op("dve", lambda e, o=o, i_=i_, g_=g_, h_=h_: e.tensor_scalar(o, i_, g_, h_, ALU.mult, ALU.add),
                             reads=[BK[tbank[hb]], "gsT", "shT"], writes=[HK + "_%d_%d" % (s, k)], n=nr)
            HKS = [[HK + "_%d_%d" % (s, k) for k in range(8)] for s in range(len(subs))]
            HKA = [k_ for ks in HKS for k_ in ks]

            def fm_mm(bank, off, c0):
                pe_group([lambda e, k=k: e.matmul(B[bank][:, off:off + NT], win[l][:, k, c0:c0 + 128], hTp[:, k, 0:NT],
                                                  start=(k == 0), stop=(k == 7)) for k in range(8)],
                         WK + HKA, [BK[bank]], 8 * NT)

            tmb = [(6, 7, 2), (0, 1, 3)]

            def tm(s):
                r0, nr = subs[s]
                bv, bu, bz = tmb[s]
                for (bank, c0) in ((bv, C_V), (bu, C_U), (bz, C_ZG)):
                    pe_group([lambda e, k=k, bank=bank, c0=c0, r0=r0, nr=nr: e.matmul(
                        B[bank][:nr, 0:384], hTp[:, k, r0:r0 + nr], win[l][:, k, c0:c0 + 384], start=(k == 0), stop=(k == 7))
                        for k in range(8)], WK + HKS[s], [BK[bank]], 8 * 384)
                S = "%d" % s
                P.op("dve", lambda e, s=s, nr=nr, bv=bv: e.bn_stats(bst[:nr, s, :], B[bv][:nr, 0:384]), reads=[BK[bv]], writes=["bst" + S], n=384)
                P.op("dve", lambda e, s=s, nr=nr: e.bn_aggr(mv[:nr, s, :], bst[:nr, s, :]), reads=["bst" + S], writes=["mv" + S], n=8)
                rstd_op(mv[:nr, s, 1:2], rsv[:nr, s:s + 1], ["mv" + S], ["rsv" + S])
                P.op("dve", lambda e, s=s, nr=nr, bv=bv: e.tensor_scalar(nv[s][:nr, :], B[bv][:nr, 0:384], mv[:nr, s, 0:1], rsv[:nr, s:s + 1],
                                                                      ALU.subtract, ALU.mult),
                     reads=[BK[bv], "mv" + S, "rsv" + S], writes=["nv" + S], n=384)
                P.op("pool", lambda e, s=s, nr=nr: e.tensor_mul(nv[s][:nr, :], nv[s][:nr, :], gam[:nr, l, :]), reads=["nv" + S, "gam"], writes=["nv" + S], n=384)
                if is_sample:
                    P.op("pool", lambda e, s=s, nr=nr: e.tensor_add(nv[s][:nr, :], nv[s][:nr, :], bet[:nr, l, :]), reads=["nv" + S, "bet"], writes=["nv" + S], n=384)
                    P.dma("sp", "nvo", nvs[l, seq - 2], nv[s][:nr, :], reads=["nv" + S])
                    P.op("pool", lambda e, s=s, nr=nr: e.tensor_copy(vnb[s][:nr, :], nv[s][:nr, :]), reads=["nv" + S], writes=["vnb" + S], n=384)
                else:
                    P.op("pool", lambda e, s=s, nr=nr: e.tensor_add(vnb[s][:nr, :], nv[s][:nr, :], bet[:nr, l, :]), reads=["nv" + S, "bet"], writes=["vnb" + S], n=384)
                P.op("act", gas(sgug[s][:nr, :], B[bz][:nr, 0:384]), reads=[BK[bz]], writes=["sgug" + S], n=384)
                P.op("dve", lambda e, s=s, nr=nr, bu=bu: e.scalar_tensor_tensor(out=sgug[s][:nr, :], in0=B[bu][:nr, 0:384], scalar=GS, in1=sgug[s][:nr, :],
                                                                            op0=ALU.mult, op1=ALU.mult),
                     reads=[BK[bu], "sgug" + S], writes=["sgug" + S], n=384)

            def pool_chunks():
                for j in range(2):
                    bx = 4 + j
                    fm_mm(bx, 0, C_P + j * 128)
                    fm_mm(bx, 256, C_ZP + j * 128)
                for j in range(2):
                    bx = 4 + j
                    pk = "pbuf%d" % j
                    W = H + NT
                    pj = pbuf[:, j, :]
                    P.op("pool", lambda e, j=j: e.tensor_copy(pbuf[:, j, 0:H], phist[:, l, j, :]), reads=["phist%d%d" % (l, j)], writes=[pk], n=15)
                    P.op("act", lambda e, j=j, bx=bx: e.copy(pbuf[:, j, H:H + NT], B[bx][:, 0:NT]), reads=[BK[bx]], writes=[pk], n=NT)
                    P.op("act", gas(sgz[:, j, 0:NT], B[bx][:, 256:256 + NT]), reads=[BK[bx]], writes=["sgz%d" % j], n=NT)
                    P.op("pool", lambda e, j=j: e.tensor_copy(phist[:, l, j, :], pbuf[:, j, NT:NT + H]), reads=[pk], writes=["phist%d%d" % (l, j)], n=15)
                    sAj, sBj = sA[:, 0, :], sB[:, 0, :]
                    ak, bk_ = "sA", "sB"
                    P.op("pool", lambda e, pj=pj, W=W, sAj=sAj: e.tensor_add(sAj[:, 1:W], pj[:, 1:W], pj[:, 0:W - 1]), reads=[pk], writes=[ak], n=W)
                    if j == 0:
                        P.op("pool", lambda e, W=W, sAj=sAj, sBj=sBj: e.tensor_add(sBj[64:128, 3:W], sAj[64:128, 3:W], sAj[64:128, 1:W - 2]),
                             reads=[ak], writes=[bk_], n=W)
                    else:
                        P.op("pool", lambda e, W=W, sAj=sAj, sBj=sBj: e.tensor_add(sBj[:, 3:W], sAj[:, 3:W], sAj[:, 1:W - 2]), reads=[ak], writes=[bk_], n=W)
                        P.op("pool", lambda e, W=W, sAj=sAj, sBj=sBj: e.tensor_add(sAj[:, 7:W], sBj[:, 7:W], sBj[:, 3:W - 4]), reads=[bk_], writes=[ak], n=W)
                        P.op("pool", lambda e, W=W, sAj=sAj, sBj=sBj: e.tensor_add(sBj[64:128, 15:W], sAj[64:128, 15:W], sAj[64:128, 7:W - 8]),
                             reads=[ak], writes=[bk_], n=W)
                    for (h0, src, skey) in ((0, sAj, ak), (64, sBj, bk_)):
                        P.op("dve", lambda e, h0=h0, src=src, j=j, pj=pj: e.scalar_tensor_tensor(
                            out=dT[h0:h0 + 64, j, 0:NT], in0=src[h0:h0 + 64, H:H + NT], scalar=invw[h0:h0 + 64, j:j + 1],
                            in1=pj[h0:h0 + 64, H:H + NT], op0=ALU.mult, op1=ALU.subtract),
                            reads=[skey, pk, "invw"], writes=["dT%d_%d" % (j, h0)], n=NT)
                        if first:
                            n1 = min(H, NT)
                            P.op("dve", lambda e, h0=h0, src=src, j=j, n1=n1: e.tensor_mul(
                                src[h0:h0 + 64, H:H + n1], src[h0:h0 + 64, H:H + n1], invcnt[h0:h0 + 64, j, 0:n1]),
                                reads=[skey, "invcnt"], writes=[skey], n=15)
                            P.op("dve", lambda e, h0=h0, src=src, j=j, n1=n1, pj=pj: e.tensor_sub(
                                dT[h0:h0 + 64, j, 0:n1], src[h0:h0 + 64, H:H + n1], pj[h0:h0 + 64, H:H + n1]),
                                reads=[skey, pk, "dT%d_%d" % (j, h0)], writes=["dT%d_%d" % (j, h0)], n=15)
                    if emit_state is not None:
                        pe_group([lambda e, j=j: e.transpose(B[4][0:H, 0:128], pbuf[:, j, NT:NT + H], identf[:])],
                                 [pk, "identf"], [BK[4]], 512)
                        P.op("dve", lambda e, j=j: e.tensor_copy(stgp[0:H, j * 128:(j + 1) * 128], B[4][0:H, 0:128]),
                             reads=[BK[4]], writes=["hxs0"], n=128)
                if emit_state is not None:
                    P.dma("sp", "stop", emit_state[0], stgp[0:H, 0:256], reads=["hxs0"])

            def conv_chunk(j, bx, by):
                fm_mm(bx, 0, C_GC + j * 128)
                fm_mm(bx, 256, C_HX + j * 128)
                fm_mm(by, 0, C_GB + j * 128)
                fm_mm(by, 256, C_ZC + j * 128)
                cp = 0
                C = "%d" % cp
                hx_, q_, ac_, sg_ = hxs[:, cp, :], qbuf[:, cp, :], acc[:, cp, :], sgc[:, cp, :]
                P.op("act", lambda e: e.copy(hx_[:, 0:NT], B[bx][:, 256:256 + NT]), reads=[BK[bx]], writes=["hxs" + C], n=NT)
                P.op("act", gas(sg_[:, 0:NT], B[by][:, 256:256 + NT]), reads=[BK[by]], writes=["sgc" + C], n=NT)
                P.op("dve", lambda e: e.tensor_mul(sg_[:, 0:NT], B[by][:, 0:NT], sg_[:, 0:NT]), reads=[BK[by], "sgc" + C], writes=["sgc" + C], n=NT)
                P.op("pool", lambda e: e.tensor_copy(q_[:, 0:2], qhist[:, l, j, :]), reads=["qhist%d%d" % (l, j)], writes=["qbuf" + C], n=2)
                P.op("dve", lambda e: e.tensor_mul(q_[:, 2:2 + NT], B[bx][:, 0:NT], hx_[:, 0:NT]),
                     reads=[BK[bx], "hxs" + C, "qbuf" + C], writes=["qbuf" + C], n=NT)
                P.op("pool", lambda e: e.tensor_copy(qhist[:, l, j, :], q_[:, NT:NT + 2]), reads=["qbuf" + C], writes=["qhist%d%d" % (l, j)], n=2)
                P.op("act", lambda e: e.activation(out=ac_[:, 0:NT], in_=q_[:, 0:NT], func=AF.Copy, scale=wc[:, l, j, 0:1]),
                     reads=["qbuf" + C, "wc"], writes=["acc" + C], n=NT)
                for tp in (1, 2):
                    P.op("dve", lambda e, tp=tp: e.scalar_tensor_tensor(
                        out=ac_[:, 0:NT], in0=q_[:, tp:tp + NT], scalar=wc[:, l, j, tp:tp + 1], in1=ac_[:, 0:NT],
                        op0=ALU.mult, op1=ALU.add), reads=["qbuf" + C, "wc", "acc" + C], writes=["acc" + C], n=NT)
                P.op("pool", lambda e: e.tensor_mul(ycTp[:, 2 + j, 0:NT], ac_[:, 0:NT], sg_[:, 0:NT]),
                     reads=["acc" + C, "sgc" + C], writes=[YK + "_%d" % (2 + j)], n=NT)
                if emit_state is not None:
                    pe_group([lambda e: e.transpose(B[4][0:2, 0:128], q_[:, NT:NT + 2], identf[:])],
                             ["qbuf" + C, "identf"], [BK[4]], 512)
                    P.op("dve", lambda e: e.tensor_copy(stgc[0:2, j * 128:(j + 1) * 128], B[4][0:2, 0:128]),
                         reads=[BK[4]], writes=["pbuf0", "pbuf1"], n=128)

            def mix(s):
                r0, nr = subs[s]
                bm = 2 + s
                S = "%d" % s
                pe_group([lambda e, h=h, nr=nr, bm=bm: e.matmul(B[bm][:nr, h * 64:(h + 1) * 64], WT[l][:nr, h, :nr],
                                                                vnb[s][:nr, h * 64:(h + 1) * 64], start=True, stop=True)
                          for h in range(6)], ["WT%d" % l, "vnb" + S], [BK[bm]], 6 * 64)
                P.op("dve", lambda e, nr=nr, bm=bm: e.tensor_add(nv[s][:nr, :], B[bm][:nr, 0:384], bsb[:nr, l, :]),
                     reads=[BK[bm], "bsb"], writes=["nv" + S], n=384)
                P.op("pool", lambda e, nr=nr: e.tensor_mul(yg[s][:nr, :], nv[s][:nr, :], sgug[s][:nr, :]),
                     reads=["nv" + S, "sgug" + S], writes=["yg" + S], n=384)

            def ytr(s, bank):
                r0, nr = subs[s]
                S = "%d" % s
                pe_group([lambda e, c=c, nr=nr: e.transpose(Bb[bank][:, c * 128:c * 128 + nr], yg[s][:nr, c * 128:(c + 1) * 128],
                                                          ident[:nr, :nr]) for c in range(3)],
                         ["yg" + S, "ident"], [BK[bank]], 384)
                P.op("act", lambda e, r0=r0, nr=nr: e.copy(ycTp[:, 5:8, r0:r0 + nr],
                                                        Bb[bank][:, 0:384].rearrange("p (c t) -> p c t", c=3)[:, :, 0:nr]),
                     reads=[BK[bank]], writes=[YK + "_g%d" % s], n=3 * nr)

            def pool_mm():
                pe_group([lambda e, j=j: e.matmul(B[4][:, j * 256:j * 256 + NT], wpbd[l][:, j, :], dT[:, j, 0:NT], start=True, stop=True)
                          for j in range(2)], ["wpbd%d" % l, "dT0_0", "dT0_64", "dT1_0", "dT1_64"], [BK[4]], 2 * NT)
                for j in range(2):
                    P.op("dve", lambda e, j=j: e.scalar_tensor_tensor(
                        out=ycTp[:, j, 0:NT], in0=B[4][:, j * 256:j * 256 + NT], scalar=psc[:, l, j:j + 1], in1=sgz[:, j, 0:NT],
                        op0=ALU.mult, op1=ALU.mult), reads=[BK[4], "sgz%d" % j, "psc"], writes=[YK + "_%d" % j], n=NT)

            tm(0)
            if len(subs) > 1:
                tm(1)
            pool_chunks()
            conv_chunk(0, 6, 7)
            mix(0)
            conv_chunk(1, 0, 1)
            pool_mm()
            if len(subs) > 1:
                mix(1)
            conv_chunk(2, 6, 7)
            if emit_state is not None:
                P.dma("sp", "stoc", emit_state[1], stgc[0:2, 0:384], reads=["pbuf0", "pbuf1"])
            ytr(0, 5)
            if len(subs) > 1:
                ytr(1, 4)

            YKA = [YK + "_%d" % c for c in range(5)] + [YK + "_g%d" % s for s in range(len(subs))]
            obanks = [(0, 1), (4, 5)]
            for s, (r0, nr) in enumerate(subs):
                xk = XK[s]
                for hf in range(2):
                    bo = obanks[s][hf]
                    korder = [0, 1, 2, 3, 5, 6, 7, 4]
                    pe_group([lambda e, k=k, i=i, bo=bo, hf=hf, r0=r0, nr=nr: e.matmul(
                        B[bo][:nr, :], ycTp[:, k, r0:r0 + nr], wout[l][:, k, hf * 512:(hf + 1) * 512], start=(i == 0), stop=(i == 7))
                        for i, k in enumerate(korder)], ["wout%d_0" % l, "wout%d_1" % l] + YKA, [BK[bo]], 8 * 512)
                    P.op("dve", lambda e, bo=bo, hf=hf, nr=nr: e.tensor_mul(B[bo][:nr, :], B[bo][:nr, :],
                                                                         gate_bc[:nr, gslot, hf * 512:(hf + 1) * 512]),
                         reads=[BK[bo], "gate_bc%d" % gslot], writes=[BK[bo]], n=512)
                    P.op("dve", lambda e, s=s, hf=hf, nr=nr, bo=bo: e.tensor_add(xb[:nr, s, hf * 512:(hf + 1) * 512],
                                                                               B[bo][:nr, :],
                                                                               xb[:nr, s, hf * 512:(hf + 1) * 512]),
                         reads=[xk, BK[bo]], writes=[xk], n=512)
                if last_layer:
                    for hf in range(2):
                        jb_ = obanks[s][hf]
                        P.op("act", lambda e, s=s, nr=nr, jb_=jb_, hf=hf: e.activation(
                            out=B[jb_][:nr, :], in_=xb[:nr, s, hf * 512:(hf + 1) * 512], func=AF.Square,
                            scale=1.0 / 32.0, accum_out=msf[:nr, 2 * s + hf:2 * s + hf + 1]),
                            reads=[xk], writes=[BK[jb_], "msf%d_%d" % (s, hf)], n=512)
                    P.op("pool", lambda e, s=s, nr=nr: e.tensor_add(rsf[:nr, s:s + 1], msf[:nr, 2 * s:2 * s + 1], msf[:nr, 2 * s + 1:2 * s + 2]),
                         reads=["msf%d_0" % s, "msf%d_1" % s], writes=["rsf%d" % s], n=1)
                    rstd_op(rsf[:nr, s:s + 1], rsf[:nr, s:s + 1], ["rsf%d" % s], ["rsf%d" % s])
                    P.op("dve", lambda e, s=s, nr=nr: e.scalar_tensor_tensor(out=xb[:nr, s, :], in0=xb[:nr, s, :],
                                                                          scalar=rsf[:nr, s:s + 1], in1=gfin[:nr, :],
                                                                          op0=ALU.mult, op1=ALU.mult),
                         reads=[xk, "rsf%d" % s, "gfin"], writes=[xk], n=1024)
                    P.dma("sp", "yout", out_ap[t0 + r0:t0 + r0 + nr, :], xb[:nr, s, :], reads=[xk], nbytes=nr * 4096)

        def gate_load(seq, l, slot, nparts=128):
            P.dma("sp", "gate%d" % slot, gate_bc[0:nparts, slot, :], gscr[l, seq:seq + 1, :].broadcast_to([nparts, D]),
                  reads=["gscr%d%d" % (l, jb) for jb in range(8, 12)], writes=["gate_bc%d" % slot], nbytes=4096 * nparts)

        def seq_begin(seq):
            for l in range(DEPTH):
                gate_load(seq, l, l)
            PH = ["phist%d%d" % (l, j) for l in range(DEPTH) for j in range(2)]
            QH = ["qhist%d%d" % (l, j) for l in range(DEPTH) for j in range(3)]
            P.op("pool", lambda e: e.memset(phist[:], 0.0), reads=PH, writes=PH, n=60)
            P.op("pool", lambda e: e.memset(qhist[:], 0.0), reads=QH, writes=QH, n=12)

        def sample_hist_load(seq, l):
            b = seq - 2
            P.dma("sp", "stip", stgp[0:15, 0:256], stp[l, b], writes=["hxs0"])
            for j in range(2):
                pe_group([lambda e, j=j: e.transpose(B[4][:, 0:15], stgp[0:15, j * 128:(j + 1) * 128], identf[0:15, 0:15])],
                         ["hxs0", "identf"], [BK[4]], 64)
                P.op("dve", lambda e, l=l, j=j: e.tensor_copy(phist[:, l, j, :], B[4][:, 0:15]),
                     reads=[BK[4]], writes=["phist%d%d" % (l, j)], n=15)
            P.dma("sp", "stic", stgc[0:2, 0:384], stc[l, b], writes=["pbuf0", "pbuf1"])
            for j in range(3):
                pe_group([lambda e, j=j: e.transpose(B[4][:, 0:2], stgc[0:2, j * 128:(j + 1) * 128], identf[0:2, 0:2])],
                         ["pbuf0", "pbuf1", "identf"], [BK[4]], 64)
                P.op("dve", lambda e, l=l, j=j: e.tensor_copy(qhist[:, l, j, :], B[4][:, 0:2]),
                     reads=[BK[4]], writes=["qhist%d%d" % (l, j)], n=2)

        import os as _os
        KSTOP = int(_os.environ.get("KSTOP", "100000"))
        _tl = tile_layer
        _cnt = [0]

        def tile_layer(*a):
            _cnt[0] += 1
            if _cnt[0] <= KSTOP:
                _tl(*a)
        ntile = SEQ // TT
        gt = 0
        for seq in range(2):
            seq_begin(seq)
            for t2 in range(0, ntile, 2):
                tl = [t for t in (t2, t2 + 1) if t < ntile]
                pps = {}
                for t in tl:
                    pp = gt % NXB
                    gt += 1
                    pps[t] = pp
                    P.dma("sp", "xin%d" % pp, xbuf[pp][:], xp[seq, t * TT:(t + 1) * TT, :].rearrange("(s p) d -> p s d", p=128),
                          writes=["x%d_0" % pp, "x%d_1" % pp], nbytes=1048576)
                for l in range(DEPTH):
                    for t in tl:
                        es = None
                        if t == ntile - 1:
                            es = (npp[l, seq], ncp[l, seq])
                        tile_layer(seq, l, TT, [(0, 128), (128, 128)], t == 0, es, False, l == DEPTH - 1, y_p[seq], t * TT, pps[t])
        spp = {}
        for seq in (2, 3):
            pp = gt % NXB
            gt += 1
            spp[seq] = pp
            P.dma("sp", "xin%d" % pp, xbuf[pp][0:NS, 0, :], xs[seq - 2], writes=["x%d_0" % pp, "x%d_1" % pp])
        for l in range(DEPTH):
            for seq in (2, 3):
                sample_hist_load(seq, l)
                gate_load(seq, l, seq - 2, nparts=NS)
                es = (nps[l, seq - 2], ncs[l, seq - 2])
                tile_layer(seq, l, NS, [(0, NS)], False, es, True, l == DEPTH - 1, y_s[seq - 2], 0, spp[seq], seq - 2)

        P.emit(reorder=REORDER)
    return nc


def make_in_maps(inputs, n_cores=8):
    g = lambda k: np.ascontiguousarray(np.asarray(inputs[k], dtype=np.float32))
    xp, xs, stp, stc = g("x_prompt"), g("x_sample"), g("state_pool"), g("state_conv")
    cp, cs = g("c_prompt"), g("c_sample")
    shared = {k: g(k) for k in ("norm_g", "w_ada", "b_ada", "w_in", "w_pool", "pool_scale", "w_conv",
                                "v_norm_g", "v_norm_b", "w_s", "b_s", "w_out")}
    shared["final_norm_g"] = g("final_norm_g").reshape(1, -1)
    maps = []
    for c in range(n_cores):
        sl = slice(2 * c, 2 * c + 2)
        m = dict(shared)
        m["xp"] = np.ascontiguousarray(xp[sl])
        m["xs"] = np.ascontiguousarray(xs[sl])
        m["stp"] = np.ascontiguousarray(stp[:, sl])
        m["stc"] = np.ascontiguousarray(stc[:, sl])
        m["c4"] = np.ascontiguousarray(np.concatenate([cp[sl], cs[sl]], axis=0))
        maps.append(m)
    return maps


def gather(results):
    cat = lambda k, ax: np.concatenate([np.asarray(r[k], dtype=np.float32) for r in results], axis=ax)
    return (cat("y_p", 0), cat("y_s", 0), cat("npp", 1), cat("ncp", 1), cat("nps", 1), cat("ncs", 1), cat("nvs", 1))


def kernel(**inputs):
    n = 8
    seq = inputs["x_prompt"].shape[1]
    ns = inputs["x_sample"].shape[1]
    nc = build_nc(seq, ns)
    res = run_bass_kernel_spmd(nc, make_in_maps(inputs, n), core_ids=list(range(n)))
    return gather(res.results)
```
